# Optimizing a Trainium2 kernel written in Bass

```python
import math
import jax, jax.numpy as jnp
from jax import lax
import numpy as np

D_MODEL = 1024
BATCH = 8
SEQ = 2048
DEPTH = 1

MLA_HEADS = 8
MLA_NOPE_DIM = 64
MLA_ROPE_DIM = 32
MLA_V_DIM = 64
MLA_QK_DIM = MLA_NOPE_DIM + MLA_ROPE_DIM
MLA_Q_RANK = 384
MLA_KV_RANK = 256
DIFF_HEADS = 4
DIFF_HEAD_DIM = 64
DIFF_V_DIM = 2 * DIFF_HEAD_DIM
MIX_WIDTH = MLA_HEADS * MLA_V_DIM + DIFF_HEADS * DIFF_V_DIM
IN_SPLITS = (MLA_Q_RANK, MLA_KV_RANK, MLA_ROPE_DIM,
             DIFF_HEADS * 2 * DIFF_HEAD_DIM,
             DIFF_HEADS * 2 * DIFF_HEAD_DIM,
             DIFF_HEADS * DIFF_V_DIM)
IN_COLS = sum(IN_SPLITS)
D_FF = 2816
CONV_WIDTH = 3
ROPE_THETA = 10000.0
NORM_EPS = 1e-6
Q_BLOCK = 128

kernel_name = "hybrid_mla_diffattn_convglu"


def rms_norm(x, g):
    xf = x.astype(jnp.float32)
    y = xf * lax.rsqrt(jnp.mean(xf * xf, axis=-1, keepdims=True) + NORM_EPS)
    return (y * g.astype(jnp.float32)).astype(x.dtype)


def rope_tables(seq, dim):
    inv = 1.0 / (ROPE_THETA ** (jnp.arange(0, dim, 2, dtype=jnp.float32) / dim))
    ang = jnp.arange(seq, dtype=jnp.float32)[:, None] * inv[None, :]
    return jnp.cos(ang), jnp.sin(ang)


def apply_rope(x, cos, sin):
    xf = x.astype(jnp.float32)
    half = xf.shape[-1] // 2
    x1, x2 = xf[..., :half], xf[..., half:]
    out = jnp.concatenate([x1 * cos - x2 * sin, x2 * cos + x1 * sin], axis=-1)
    return out.astype(x.dtype)


def causal_block_probs(q_blk, k_pre, scale, q_start):
    s = jnp.einsum('bhqd,bhkd->bhqk', q_blk, k_pre).astype(jnp.float32) * scale
    q_pos = q_start + jnp.arange(q_blk.shape[2])
    k_pos = jnp.arange(k_pre.shape[2])
    mask = k_pos[None, :] <= q_pos[:, None]
    s = jnp.where(mask, s, -jnp.inf)
    return jax.nn.softmax(s, axis=-1)


def mla_attention(q, k, v, scale):
    seq = q.shape[2]
    outs = []
    for i in range(seq // Q_BLOCK):
        s0, e = i * Q_BLOCK, (i + 1) * Q_BLOCK
        p = causal_block_probs(q[:, :, s0:e], k[:, :, :e], scale, s0)
        outs.append(jnp.einsum('bhqk,bhkd->bhqd', p.astype(v.dtype), v[:, :, :e]))
    return jnp.concatenate(outs, axis=2)


def differential_attention(q1, q2, k1, k2, v, lam, scale):
    seq = q1.shape[2]
    outs = []
    for i in range(seq // Q_BLOCK):
        s0, e = i * Q_BLOCK, (i + 1) * Q_BLOCK
        p1 = causal_block_probs(q1[:, :, s0:e], k1[:, :, :e], scale, s0)
        p2 = causal_block_probs(q2[:, :, s0:e], k2[:, :, :e], scale, s0)
        w = (p1 - lam * p2).astype(v.dtype)
        outs.append(jnp.einsum('bhqk,bhkd->bhqd', w, v[:, :, :e]))
    return jnp.concatenate(outs, axis=2)


def causal_depthwise_conv(x, w, b):
    seq = x.shape[1]
    xp = jnp.pad(x, ((0, 0), (CONV_WIDTH - 1, 0), (0, 0)))
    out = b
    for j in range(CONV_WIDTH):
        out = out + xp[:, j:j + seq] * w[j]
    return out


def setup_inputs(seed: int = 0) -> dict:
    key = jax.random.key(seed)
    ks = jax.random.split(key, 24)
    L = DEPTH

    def nrm(k, shape, fan_in):
        return jax.random.normal(k, shape, jnp.float32) * (fan_in ** -0.5)

    def gain(k, shape):
        return 1.0 + 0.02 * jax.random.normal(k, shape, jnp.float32)

    return {
        "x": jax.random.normal(ks[0], (BATCH, SEQ, D_MODEL), jnp.float32),
        "attn_norm_g": gain(ks[1], (L, D_MODEL)),
        "w_in": nrm(ks[2], (L, D_MODEL, IN_COLS), D_MODEL),
        "q_a_norm_g": gain(ks[3], (L, MLA_Q_RANK)),
        "w_q_up": nrm(ks[4], (L, MLA_Q_RANK, MLA_HEADS * MLA_QK_DIM), MLA_Q_RANK),
        "kv_a_norm_g": gain(ks[5], (L, MLA_KV_RANK)),
        "w_kv_up": nrm(ks[6], (L, MLA_KV_RANK, MLA_HEADS * (MLA_NOPE_DIM + MLA_V_DIM)), MLA_KV_RANK),
        "mla_q_norm_g": gain(ks[7], (L, MLA_QK_DIM)),
        "mla_k_norm_g": gain(ks[8], (L, MLA_QK_DIM)),
        "diff_q_norm_g": gain(ks[9], (L, DIFF_HEAD_DIM)),
        "diff_k_norm_g": gain(ks[10], (L, DIFF_HEAD_DIM)),
        "lambda_q1": 0.1 * jax.random.normal(ks[11], (L, DIFF_HEAD_DIM), jnp.float32),
        "lambda_k1": 0.1 * jax.random.normal(ks[12], (L, DIFF_HEAD_DIM), jnp.float32),
        "lambda_q2": 0.1 * jax.random.normal(ks[13], (L, DIFF_HEAD_DIM), jnp.float32),
        "lambda_k2": 0.1 * jax.random.normal(ks[14], (L, DIFF_HEAD_DIM), jnp.float32),
        "diff_subln_g": gain(ks[15], (L, DIFF_V_DIM)),
        "w_out": nrm(ks[16], (L, MIX_WIDTH, D_MODEL), MIX_WIDTH),
        "ffn_norm_g": gain(ks[17], (L, D_MODEL)),
        "w_gate": nrm(ks[18], (L, D_MODEL, D_FF), D_MODEL),
        "w_up": nrm(ks[19], (L, D_MODEL, D_FF), D_MODEL),
        "conv_w": nrm(ks[20], (L, CONV_WIDTH, D_FF), CONV_WIDTH),
        "conv_b": 0.02 * jax.random.normal(ks[21], (L, D_FF), jnp.float32),
        "w_down": nrm(ks[22], (L, D_FF, D_MODEL), D_FF),
    }


def reference(x, attn_norm_g, w_in, q_a_norm_g, w_q_up, kv_a_norm_g, w_kv_up,
              mla_q_norm_g, mla_k_norm_g, diff_q_norm_g, diff_k_norm_g,
              lambda_q1, lambda_k1, lambda_q2, lambda_k2, diff_subln_g, w_out,
              ffn_norm_g, w_gate, w_up, conv_w, conv_b, w_down):
    B, S, _ = x.shape
    cos_a, sin_a = rope_tables(S, MLA_ROPE_DIM)
    cos_b, sin_b = rope_tables(S, DIFF_HEAD_DIM)
    split_idx = []
    acc = 0
    for n in IN_SPLITS[:-1]:
        acc += n
        split_idx.append(acc)
    mla_scale = MLA_QK_DIM ** -0.5
    diff_scale = DIFF_HEAD_DIM ** -0.5

    for l in range(DEPTH):
        h = rms_norm(x, attn_norm_g[l])
        proj = h @ w_in[l]
        cq, ckv, kpe, dq, dk, dv = jnp.split(proj, split_idx, axis=-1)

        q = (rms_norm(cq, q_a_norm_g[l]) @ w_q_up[l]).reshape(B, S, MLA_HEADS, MLA_QK_DIM)
        kv = (rms_norm(ckv, kv_a_norm_g[l]) @ w_kv_up[l]).reshape(B, S, MLA_HEADS, MLA_NOPE_DIM + MLA_V_DIM)
        k_nope, v_a = kv[..., :MLA_NOPE_DIM], kv[..., MLA_NOPE_DIM:]
        gq, gk = mla_q_norm_g[l], mla_k_norm_g[l]
        q_nope = rms_norm(q[..., :MLA_NOPE_DIM], gq[:MLA_NOPE_DIM]).transpose(0, 2, 1, 3)
        q_pe = rms_norm(q[..., MLA_NOPE_DIM:], gq[MLA_NOPE_DIM:]).transpose(0, 2, 1, 3)
        k_nope = rms_norm(k_nope, gk[:MLA_NOPE_DIM]).transpose(0, 2, 1, 3)
        k_pe = rms_norm(kpe, gk[MLA_NOPE_DIM:])[:, None]
        q_pe = apply_rope(q_pe, cos_a, sin_a)
        k_pe = apply_rope(k_pe, cos_a, sin_a)
        q_a = jnp.concatenate([q_nope, q_pe], axis=-1)
        k_a = jnp.concatenate([k_nope, jnp.broadcast_to(k_pe, (B, MLA_HEADS, S, MLA_ROPE_DIM))], axis=-1)
        o_a = mla_attention(q_a, k_a, v_a.transpose(0, 2, 1, 3), mla_scale)
        o_a = o_a.transpose(0, 2, 1, 3).reshape(B, S, MLA_HEADS * MLA_V_DIM)

        dq = rms_norm(dq.reshape(B, S, DIFF_HEADS, 2, DIFF_HEAD_DIM), diff_q_norm_g[l])
        dk = rms_norm(dk.reshape(B, S, DIFF_HEADS, 2, DIFF_HEAD_DIM), diff_k_norm_g[l])
        dq = apply_rope(dq.transpose(0, 2, 3, 1, 4), cos_b, sin_b)
        dk = apply_rope(dk.transpose(0, 2, 3, 1, 4), cos_b, sin_b)
        v_b = dv.reshape(B, S, DIFF_HEADS, DIFF_V_DIM).transpose(0, 2, 1, 3)
        lam_init = 0.8 - 0.6 * math.exp(-0.3 * l)
        lam = (jnp.exp(jnp.sum(lambda_q1[l].astype(jnp.float32) * lambda_k1[l].astype(jnp.float32)))
               - jnp.exp(jnp.sum(lambda_q2[l].astype(jnp.float32) * lambda_k2[l].astype(jnp.float32)))
               + lam_init)
        o_b = differential_attention(dq[:, :, 0], dq[:, :, 1], dk[:, :, 0], dk[:, :, 1], v_b, lam, diff_scale)
        o_b = rms_norm(o_b, diff_subln_g[l]) * (1.0 - lam_init)
        o_b = o_b.transpose(0, 2, 1, 3).reshape(B, S, DIFF_HEADS * DIFF_V_DIM)

        mix = jnp.concatenate([o_a, o_b], axis=-1) @ w_out[l]
        x = x + mix

        h = rms_norm(x, ffn_norm_g[l])
        g = causal_depthwise_conv(h @ w_gate[l], conv_w[l], conv_b[l])
        u = h @ w_up[l]
        x = x + (jax.nn.silu(g) * u) @ w_down[l]
    return x
```

```python
import math
import numpy as np
import ml_dtypes
import concourse.bass as bass
import concourse.mybir as mybir
from concourse.bass_utils import run_bass_kernel_spmd

F32 = mybir.dt.float32
BF16 = mybir.dt.bfloat16
ALU = mybir.AluOpType
AF = mybir.ActivationFunctionType
AX = mybir.AxisListType
ESZ = {F32: 4, BF16: 2}

S = 2048
D = 1024
NT = 16
INC = 2208
DFF = 2816
NF = 22
EPS = 1e-6
LAM_INIT = 0.8 - 0.6 * math.exp(-0.3 * 0)
MLA_SCALE = 96 ** -0.5
DIFF_SCALE = 64 ** -0.5
NEG = -30000.0

STOP = None
REORDER = False
PREP_ENG = "dve"
DBL_1B = True
ILV_1B = True
SKEW_1B = False
WBANK = True
SKIP_A = False
DUMPS = []


class _Rec:
    __slots__ = ("p0", "p1", "b0", "b1", "ops", "w", "eng", "dead", "dma")


class _Op:
    __slots__ = ("id", "eng", "fn", "deps", "dma", "need", "sig", "dsem", "dtgt", "dprev", "cost", "lat", "pos", "waits")


class Sched:
    BK = 2048
    HOP = 900.0
    HOP_SAME = 250.0
    WIN = {"pe": 1, "act": 48, "dve": 64, "pool": 24, "sp": 12}

    def __init__(self):
        self.ops = []
        self.buckets = {}
        self.rkey = {}

    @staticmethod
    def rect(ap):
        name = ap.tensor.name
        esz = ESZ[ap.dtype]
        apl = ap.ap
        off = ap.offset
        if "DRAM" in str(ap.space):
            lo = hi = 0
            for s, c in apl:
                if s >= 0:
                    hi += s * (c - 1)
                else:
                    lo += s * (c - 1)
            return (name, 0, 1, (off + lo) * esz, (off + hi + 1) * esz)
        ps, pn = apl[0]
        p0 = off // ps
        col = off % ps
        lo = hi = 0
        for s, c in apl[1:]:
            if s >= 0:
                hi += s * (c - 1)
            else:
                lo += s * (c - 1)
        return (name, p0, p0 + pn, (col + lo) * esz, (col + hi + 1) * esz)

    def add(self, eng, fn, reads, writes, dma=False, cost=100.0, lat=0.0, wrects=()):
        op = _Op()
        op.id = len(self.ops)
        op.eng = eng
        op.fn = fn
        op.dma = dma
        op.need = False
        op.sig = None
        op.dsem = None
        op.cost = cost
        op.lat = lat
        deps = {}
        ops = self.ops

        def note(rec, raw):
            for d in rec.ops:
                if d == op.id:
                    continue
                dop = ops[d]
                sem = True
                if not dma and not dop.dma and dop.eng == eng:
                    sem = eng != "pe"
                if sem or d not in deps:
                    deps[d] = sem or deps.get(d, False)

        rr = [self.rect(a) for a in reads]
        ww = [self.rect(a) for a in writes] + list(wrects)
        BK = self.BK
        for (name, p0, p1, b0, b1) in rr:
            for bk in range(b0 // BK, (b1 - 1) // BK + 1):
                lst = self.buckets.get((name, bk))
                if not lst:
                    continue
                for rec in lst:
                    if rec.w and not rec.dead and rec.p0 < p1 and p0 < rec.p1 and rec.b0 < b1 and b0 < rec.b1:
                        note(rec, True)
        for (name, p0, p1, b0, b1) in ww:
            for bk in range(b0 // BK, (b1 - 1) // BK + 1):
                lst = self.buckets.get((name, bk))
                if not lst:
                    continue
                alive = []
                for rec in lst:
                    if rec.dead:
                        continue
                    if rec.p0 < p1 and p0 < rec.p1 and rec.b0 < b1 and b0 < rec.b1:
                        note(rec, False)
                        if rec.p0 >= p0 and rec.p1 <= p1 and rec.b0 >= b0 and rec.b1 <= b1:
                            rec.dead = True
                            continue
                    alive.append(rec)
                self.buckets[(name, bk)] = alive
        for (name, p0, p1, b0, b1) in rr:
            key = (name, eng, dma, p0, p1, b0, b1)
            rec = self.rkey.get(key)
            if rec is not None and not rec.dead:
                if rec.ops[-1] != op.id:
                    rec.ops.append(op.id)
                continue
            rec = _Rec()
            rec.p0, rec.p1, rec.b0, rec.b1, rec.ops, rec.w, rec.eng, rec.dead, rec.dma = p0, p1, b0, b1, [op.id], False, eng, False, dma
            self.rkey[key] = rec
            for bk in range(b0 // BK, (b1 - 1) // BK + 1):
                self.buckets.setdefault((name, bk), []).append(rec)
        for (name, p0, p1, b0, b1) in ww:
            rec = _Rec()
            rec.p0, rec.p1, rec.b0, rec.b1, rec.ops, rec.w, rec.eng, rec.dead, rec.dma = p0, p1, b0, b1, [op.id], True, eng, False, dma
            for bk in range(b0 // BK, (b1 - 1) // BK + 1):
                self.buckets.setdefault((name, bk), []).append(rec)
        op.deps = deps
        self.ops.append(op)
        return op

    def schedule(self):
        ops = self.ops
        n = len(ops)
        engs = ["pe", "act", "dve", "pool", "sp"]
        queue = {e: [op.id for op in ops if op.eng == e] for e in engs}
        ptr = {e: 0 for e in engs}
        sched = [False] * n
        done = [0.0] * n
        pending = [len(op.deps) for op in ops]
        ready_t = [0.0] * n
        users = [[] for _ in range(n)]
        for op in ops:
            for d, sem in op.deps.items():
                users[d].append((op.id, sem))
        efree = {e: 0.0 for e in engs}
        order = {e: [] for e in engs}
        remaining = n
        HOP, HOPS = self.HOP, self.HOP_SAME
        while remaining:
            best = None
            for e in engs:
                q = queue[e]
                i = ptr[e]
                lq = len(q)
                while i < lq and sched[q[i]]:
                    i += 1
                ptr[e] = i
                cnt = 0
                j = i
                W = self.WIN[e]
                ef = efree[e]
                while j < lq and cnt < W:
                    oid = q[j]
                    j += 1
                    if sched[oid]:
                        continue
                    cnt += 1
                    if pending[oid]:
                        continue
                    rt = ready_t[oid]
                    st = rt if rt > ef else ef
                    score = st + 3.0 * (cnt - 1)
                    if best is None or score < best[0]:
                        best = (score, st, e, oid)
                    if st <= ef:
                        break
            score, st, e, oid = best
            op = ops[oid]
            sched[oid] = True
            efree[e] = st + op.cost
            dn = st + op.cost + op.lat
            done[oid] = dn
            for (u, sem) in users[oid]:
                pending[u] -= 1
                if sem:
                    t = dn + (HOPS if (ops[u].eng == e and not op.dma and not ops[u].dma) else HOP)
                    if t > ready_t[u]:
                        ready_t[u] = t
            op.pos = len(order[e])
            order[e].append(op)
            remaining -= 1
        self.model_time = max(done) if n else 0.0
        return order

    def emit(self, nc, stack, reorder=True):
        ops = self.ops
        engs = ["pe", "act", "dve", "pool", "sp"]
        if reorder:
            per = self.schedule()
        else:
            per = {e: [op for op in ops if op.eng == e] for e in engs}
            for e in engs:
                for i, op in enumerate(per[e]):
                    op.pos = i
        for op in ops:
            best = {}
            dmas = []
            for d, sem in op.deps.items():
                if not sem:
                    continue
                dop = ops[d]
                if dop.dma:
                    dmas.append(dop)
                elif dop.eng not in best or best[dop.eng].pos < dop.pos:
                    best[dop.eng] = dop
            op.waits = (dmas, list(best.values()))
            for dop in best.values():
                dop.need = True
        SEG = 1000
        esem = {}
        for e in engs:
            c = 0
            for op in per[e]:
                if not op.dma and op.need:
                    k = c // SEG
                    if (e, k) not in esem:
                        esem[(e, k)] = stack.enter_context(nc.semaphore("s_%s_%d" % (e, k)))
                    op.sig = (esem[(e, k)], c % SEG + 1)
                    c += 1
        NDS = {"sp": 24, "pool": 12, "act": 1, "pe": 1, "dve": 1}
        dsems = {e: [stack.enter_context(nc.semaphore("d_%s_%d" % (e, i))) for i in range(NDS[e])] for e in engs}
        dcnt = {e: [0] * NDS[e] for e in engs}
        for e in engs:
            k = 0
            for op in per[e]:
                if op.dma:
                    i = k % NDS[e]
                    k += 1
                    op.dsem = dsems[e][i]
                    op.dprev = dcnt[e][i]
                    dcnt[e][i] += 16
                    op.dtgt = dcnt[e][i]
        block = stack.enter_context(nc.Block())

        def run(e, eng):
            waited = {}

            def wait(sem, val):
                k = id(sem)
                if waited.get(k, 0) >= val:
                    return
                waited[k] = val
                eng.wait_ge(sem, val)

            for op in per[e]:
                dmas, comps = op.waits
                for dop in dmas:
                    wait(dop.dsem, dop.dtgt)
                for dop in comps:
                    wait(dop.sig[0], dop.sig[1])
                if op.dma and op.dprev > 0:
                    wait(op.dsem, op.dprev)
                inst = op.fn(eng)
                if op.dma:
                    inst.then_inc(op.dsem, 16)
                elif op.need:
                    inst.then_inc(op.sig[0], 1)
            for i, sem in enumerate(dsems[e]):
                if dcnt[e][i] > 0:
                    wait(sem, dcnt[e][i])

        @block.tensor
        def _(eng):
            run("pe", eng)

        @block.scalar
        def _(eng):
            run("act", eng)

        @block.vector
        def _(eng):
            run("dve", eng)

        @block.gpsimd
        def _(eng):
            run("pool", eng)

        @block.sync
        def _(eng):
            run("sp", eng)


class Buf:
    def __init__(self, h, rowlen, col0):
        self.h, self.rowlen, self.col0 = h, rowlen, col0

    def __call__(self, off=0, dims=((1, 1),), p0=0, np_=128):
        return bass.AP(self.h, p0 * self.rowlen + self.col0 + off, [[self.rowlen, np_]] + [list(d) for d in dims])


class Builder:
    SB_BYTES = 192 * 1024

    def __init__(self):
        self.nc = bass.Bass("TRN2", target_bir_lowering=False)
        self.s = Sched()
        self.dump_specs = []
        self._cap = None
        self.wb_next = set()

    def _add(self, *a, **kw):
        if self._cap is not None:
            self._cap.append((a, kw))
        else:
            self.s.add(*a, **kw)

    def capture(self, f, *args):
        old = self._cap
        self._cap = []
        f(*args)
        lst = self._cap
        self._cap = old
        return lst

    def interleave(self, lists):
        lists = [l for l in lists if l]
        idx = [0] * len(lists)
        while True:
            best = None
            for k, l in enumerate(lists):
                if idx[k] < len(l):
                    fr = idx[k] / len(l)
                    if best is None or fr < best[0]:
                        best = (fr, k)
            if best is None:
                break
            k = best[1]
            a, kw = lists[k][idx[k]]
            idx[k] += 1
            self._add(*a, **kw)

    @staticmethod
    def nfree(ap):
        n = 1
        for s, c in ap.ap[1:]:
            n *= c
        return n

    def bank_rect(self, out):
        name, p0, p1, b0, b1 = Sched.rect(out)
        bk = b0 // 2048
        return (name, 0, 128, bk * 2048, bk * 2048 + 2048)

    def mm(self, out, lhsT, rhs, start=True, stop=True, skip=False):
        n = self.nfree(rhs)
        wr = ()
        if start:
            br = self.bank_rect(out)
            bk = br[3] // 2048
            if WBANK == 2 or bk in self.wb_next:
                self.wb_next.discard(bk)
                wr = (br,)
        self._add("pe", lambda e: e.matmul(out, lhsT, rhs, start=start, stop=stop, skip_group_check=skip),
                   [lhsT, rhs], [out], cost=max(n, 64) * 0.42 + 35.0, lat=60.0, wrects=wr)

    def tr(self, out, in_, ident=None):
        ident = ident if ident is not None else self.ident_ap
        self._add("pe", lambda e: e.transpose(out, in_, ident), [in_, ident], [out], cost=75.0, lat=60.0,
                   wrects=(self.bank_rect(out),) if WBANK else ())

    def act(self, out, in_, func, scale=None, bias=None, accum=None):
        reads = [in_]
        kw = {}
        if scale is not None:
            kw["scale"] = scale
            if not isinstance(scale, (int, float)):
                reads.append(scale)
        if bias is not None:
            kw["bias"] = bias
            if not isinstance(bias, (int, float)):
                reads.append(bias)
        writes = [out]
        if accum is not None:
            kw["accum_out"] = accum
            writes.append(accum)
        cost = 190.0 + self.nfree(in_) * 0.73 + (95.0 if accum is not None else 0.0)
        self._add("act", lambda e: e.activation(out, in_, func, **kw), reads, writes, cost=cost, lat=50.0)

    def _ecost(self, eng, n):
        if eng == "pool":
            return 250.0 + n * 1.45
        return 75.0 + n * 1.05

    def tt(self, out, a, b, op, eng="dve"):
        self._add(eng, lambda e: e.tensor_tensor(out, a, b, op), [a, b], [out],
                   cost=self._ecost(eng, self.nfree(out)), lat=50.0)

    def ts(self, out, a, s1, op0, s2=None, op1=None, eng="dve"):
        reads = [a]
        if not isinstance(s1, (int, float)):
            reads.append(s1)
        if s2 is not None and not isinstance(s2, (int, float)):
            reads.append(s2)
        c = self._ecost(eng, self.nfree(out))
        if op1 is None:
            self._add(eng, lambda e: e.tensor_scalar(out, a, s1, None, op0), reads, [out], cost=c, lat=50.0)
        else:
            self._add(eng, lambda e: e.tensor_scalar(out, a, s1, s2, op0, op1), reads, [out], cost=c, lat=50.0)

    def stt(self, out, a, sc, b, op0, op1):
        reads = [a, b]
        if not isinstance(sc, (int, float)):
            reads.append(sc)
        self._add("dve", lambda e: e.scalar_tensor_tensor(out, a, sc, b, op0, op1), reads, [out],
                   cost=self._ecost("dve", self.nfree(out)), lat=50.0)

    def red(self, out, in_, op=ALU.add):
        self._add("dve", lambda e: e.tensor_reduce(out, in_, AX.X, op), [in_], [out],
                   cost=self._ecost("dve", self.nfree(in_)), lat=50.0)

    def recip(self, out, in_):
        self._add("dve", lambda e: e.reciprocal(out, in_), [in_], [out],
                   cost=75.0 + 8.4 * self.nfree(out), lat=50.0)

    def cp(self, out, in_, eng="dve"):
        if eng == "act":
            self._add("act", lambda e: e.copy(out, in_), [in_], [out],
                       cost=190.0 + self.nfree(in_) * 0.73, lat=50.0)
        else:
            self._add(eng, lambda e: e.tensor_copy(out, in_), [in_], [out],
                       cost=self._ecost(eng, self.nfree(out)), lat=50.0)

    def memset(self, ap, val, eng="pool"):
        self._add(eng, lambda e: e.memset(ap, val), [], [ap], cost=self._ecost(eng, self.nfree(ap)), lat=50.0)

    def dma(self, out, in_, q="sp", **kw):
        nbytes = self.nfree(out) * out.ap[0][1] * ESZ[out.dtype] if "DRAM" not in str(out.space) else \
            self.nfree(in_) * in_.ap[0][1] * ESZ[in_.dtype]
        if q == "sp":
            cost, lat = 70.0, 2200.0 + nbytes / 120.0
        else:
            cost, lat = 700.0, 2600.0 + nbytes / 90.0
        self._add(q, lambda e: e.dma_start(out, in_, **kw), [in_], [out], dma=True, cost=cost, lat=lat)

    def rsqrt(self, out, ss, tmp, inv_n, extra_bias=None):
        self.act(tmp, ss, AF.Ln, scale=inv_n, bias=self.cst(0, [(1, 1)]))
        if extra_bias is None:
            self.act(out, tmp, AF.Exp, scale=-0.5)
        else:
            self.act(out, tmp, AF.Exp, scale=-0.5, bias=extra_bias)

    def build(self):
        import contextlib
        nc = self.nc
        dt = nc.dram_tensor
        self.x_d = dt("x", [S, D], F32, kind="ExternalInput")
        self.win_d = dt("w_in", [D, INC], F32, kind="ExternalInput")
        self.wq_d = dt("w_q_up", [384, 768], F32, kind="ExternalInput")
        self.wkv_d = dt("w_kv_up", [256, 1024], F32, kind="ExternalInput")
        self.wout_d = dt("w_out", [D, D], F32, kind="ExternalInput")
        self.wg_d = dt("w_gate", [D, DFF], F32, kind="ExternalInput")
        self.wu_d = dt("w_up", [D, DFF], F32, kind="ExternalInput")
        self.wd_d = dt("w_down", [DFF, D], F32, kind="ExternalInput")
        self.pvec_d = dt("pvec", [128, 112], F32, kind="ExternalInput")
        self.bvec_d = dt("bvec", [128, 576], F32, kind="ExternalInput")
        self.ropeA_d = dt("ropeA", [S, 64], F32, kind="ExternalInput")
        self.ropeB_d = dt("ropeB", [S, 128], F32, kind="ExternalInput")
        self.ident_d = dt("ident", [128, 128], BF16, kind="ExternalInput")
        self.mask_d = dt("maskT", [128, 128], BF16, kind="ExternalInput")
        self.y_d = dt("y", [S, D], F32, kind="ExternalOutput")
        with contextlib.ExitStack() as stack:
            big = stack.enter_context(nc.sbuf_tensor("big", [128, self.SB_BYTES // 4], F32))
            ps = stack.enter_context(nc.psum_tensor("ps", [128, 4096], F32))
            self.big, self.ps = big, ps
            self.bigb = big.bitcast(BF16)
            self.psb = ps.bitcast(BF16)
            self.program()
            self.s.emit(nc, stack, reorder=REORDER)
        return nc

    def f32(self, byte_off):
        assert byte_off % 4 == 0
        return Buf(self.big, self.SB_BYTES // 4, byte_off // 4)

    def bf(self, byte_off):
        assert byte_off % 2 == 0
        return Buf(self.bigb, self.SB_BYTES // 2, byte_off // 2)

    def bank(self, b):
        return Buf(self.ps, 4096, b * 512)

    def bankb(self, b):
        return Buf(self.psb, 8192, b * 1024)

    def dram(self, h, off, dims):
        return bass.AP(h, off, [list(d) for d in dims])

    def dump(self, name, ap, shape):
        h = self.nc.dram_tensor("dbg_" + name, list(shape), ap.dtype, kind="ExternalOutput")
        self.dma(h.ap(), ap)
        self.dump_specs.append("dbg_" + name)

    def program(self):
        KB = 1024
        top = 188 * KB
        self.cst = self.f32(top)
        self.pvec = self.f32(top + 64)
        self.bvec = self.f32(top + 64 + 448)
        identb = self.bf(top + 64 + 448 + 2304)
        maskb = self.bf(top + 64 + 448 + 2304 + 256)
        lamb = self.f32(top + 64 + 448 + 2304 + 512)
        self.ident_ap = identb(0, [(1, 128)])
        self.mask_ap = maskb(0, [(1, 128)])
        cst, pvec, bvec = self.cst, self.pvec, self.bvec

        self.memset(cst(0, [(1, 1)]), EPS, eng="dve")
        self.memset(cst(1, [(1, 1)]), math.log(1.0 - LAM_INIT), eng="dve")
        self.dma(pvec(0, [(1, 112)]), self.pvec_d.ap())
        self.dma(bvec(0, [(1, 576)]), self.bvec_d.ap())
        self.dma(self.ident_ap, self.ident_d.ap())
        self.dma(self.mask_ap, self.mask_d.ap())

        ltmp = self.f32(176 * KB)
        self.tt(ltmp(0, [(1, 64)]), bvec(320, [(1, 64)]), bvec(384, [(1, 64)]), ALU.mult)
        self.red(lamb(0, [(1, 1)]), ltmp(0, [(1, 64)]))
        self.tt(ltmp(64, [(1, 64)]), bvec(448, [(1, 64)]), bvec(512, [(1, 64)]), ALU.mult)
        self.red(lamb(1, [(1, 1)]), ltmp(64, [(1, 64)]))
        self.act(lamb(2, [(1, 2)]), lamb(0, [(1, 2)]), AF.Exp)
        self.tt(lamb(4, [(1, 1)]), lamb(2, [(1, 1)]), lamb(3, [(1, 1)]), ALU.subtract)
        self.ts(lamb(5, [(1, 1)]), lamb(4, [(1, 1)]), -1.0, ALU.mult, -LAM_INIT, ALU.add)
        self.neg_lam = lamb(5, [(1, 1)])

        R_X = 0
        R_H = 64 * KB
        R_OA = 96 * KB
        R_P = 112 * KB
        self.hT = self.bf(R_H)

        self.phase1a(R_X, R_OA, R_P)
        if STOP in ("p0", "p1a", "p1a_prep"):
            return
        self.phase1b(R_X, R_H, R_OA, R_P)
        if STOP in ("p1b", "p1b_prep"):
            return
        self.phase2(R_X, R_H, R_OA)

    def norm_tile(self, t, xa, hT, gcol, xb, st, pb, load=None):
        if load is not None:
            self.dma(xa, load)
        self.act(xb(0, [(1, 1024)]), xa, AF.Square, accum=st(t, [(1, 1)]))
        self.rsqrt(st(32 + t, [(1, 1)]), st(t, [(1, 1)]), st(16 + t, [(1, 1)]), 1.0 / D)
        self.act(xb(0, [(1, 1024)]), xa, AF.Copy, scale=st(32 + t, [(1, 1)]))
        for c in range(8):
            self.tr(self.bankb(pb)(c * 128, [(1, 128)]), xb(c * 128, [(1, 128)]))
        self.tt(hT(t * 128, [(2048, 8), (1, 128)]), self.bankb(pb)(0, [(128, 8), (1, 128)]),
                self.pvec(gcol, [(1, 8), (0, 128)]), ALU.mult)

    def load_w(self, dst, src_h, row0, nrows_chunks, col0, ncols, ld, dst_cstride):
        for c in range(nrows_chunks):
            c0 = 0
            while c0 < ncols:
                n = min(1024, ncols - c0)
                self.dma(dst(c * dst_cstride + c0, [(1, n)]),
                         self.dram(src_h, (row0 + c * 128) * ld + col0 + c0, [(ld, 128), (1, n)]), q="pool")
                c0 += n

    def phase1a(self, R_X, R_OA, R_P):
        KB = 1024
        hT, pvec, bvec = self.hT, self.pvec, self.bvec
        qT = self.bf(R_X)
        kT = self.bf(R_X + 32 * KB)
        oaT = self.bf(R_OA)
        xs = [self.f32(R_OA), self.f32(R_OA + 4 * KB)]
        xb0 = [self.bf(R_OA + 8 * KB), self.bf(R_OA + 10 * KB)]
        st0 = self.f32(R_OA + 12 * KB)
        o = R_P
        win = self.bf(o); o += 8 * 672 * 2
        wq = self.bf(o); o += 3 * 768 * 2
        wkv = self.bf(o); o += 2 * 1024 * 2
        vA = self.bf(o); o += 16 * 8 * 66 * 2
        ropeA = self.f32(o); o += 16 * 64 * 4
        tbase = o
        T = []
        for k in range(2):
            d = {}
            d["cqb"] = self.bf(o); o += 640 * 2
            d["cT"] = self.bf(o); o += 640 * 2
            d["junk"] = self.bf(o); o += 672 * 2
            d["sqq"] = self.f32(o); o += 768 * 4
            d["sqk"] = self.f32(o); o += 512 * 4
            d["qasm"] = self.bf(o); o += 768 * 2
            d["kasm"] = self.bf(o); o += 768 * 2
            d["tmpq"] = self.f32(o); o += 768 * 4
            d["tmpk"] = self.f32(o); o += 96 * 4
            d["st"] = self.f32(o); o += 128 * 4
            T.append(d)
        ropeAq = self.f32(o); o += 16 * 64 * 4
        assert o <= 188 * KB, o
        o2 = tbase
        PT = [self.bf(o2), self.bf(o2 + 2 * KB)]; o2 += 4 * KB
        ocat = self.bf(o2); o2 += 4 * 512 * 2
        rinv = self.f32(o2); o2 += 128
        self.vA = vA

        self.load_w(win, self.win_d, 0, 8, 0, 672, INC, 672)
        self.load_w(wq, self.wq_d, 0, 3, 0, 768, 768, 768)
        self.load_w(wkv, self.wkv_d, 0, 2, 0, 1024, 1024, 1024)
        self.dma(ropeA(0, [(64, 16), (1, 64)]), self.dram(self.ropeA_d, 0, [(64, 128), (128 * 64, 16), (1, 64)]))
        self.tt(ropeAq(0, [(64, 16), (1, 32)]), ropeA(0, [(64, 16), (1, 32)]), bvec(0, [(0, 16), (1, 32)]), ALU.mult)
        self.tt(ropeAq(32, [(64, 16), (16, 2), (1, 16)]), ropeA(32, [(64, 16), (16, 2), (1, 16)]),
                bvec(16, [(0, 16), (-16, 2), (1, 16)]), ALU.mult)
        self.memset(vA(64, [(66, 128), (1, 2)]), 1.0)

        def p0(t):
            self.norm_tile(t, xs[t % 2](0, [(1, 1024)]), hT, 0, xb0[t % 2], st0, 7,
                           load=self.dram(self.x_d, t * 128 * D, [(D, 128), (1, D)]))

        eps = self.cst(0, [(1, 1)])

        def bufs(t):
            d = T[t % 2]
            return tuple(d[k] for k in ("cqb", "cT", "junk", "sqq", "sqk", "qasm", "kasm", "tmpq", "tmpk", "st"))

        def st1(t):
            tb = t * 128
            cqb, cT, junk, sqq, sqk, qasm, kasm, tmpq, tmpk, st = bufs(t)
            for c in range(8):
                self.mm(self.bank(0)(0, [(1, 384)]), hT(c * 2048 + tb, [(1, 128)]), win(c * 672, [(1, 384)]),
                        start=(c == 0), stop=(c == 7))
            for c in range(8):
                self.mm(self.bank(1)(0, [(1, 288)]), hT(c * 2048 + tb, [(1, 128)]), win(c * 672 + 384, [(1, 288)]),
                        start=(c == 0), stop=(c == 7))
            self.act(junk(0, [(1, 384)]), self.bank(0)(0, [(1, 384)]), AF.Square, accum=st(0, [(1, 1)]))
            self.act(junk(384, [(1, 256)]), self.bank(1)(0, [(1, 256)]), AF.Square, accum=st(1, [(1, 1)]))
            self.act(junk(640, [(1, 32)]), self.bank(1)(256, [(1, 32)]), AF.Square, accum=st(2, [(1, 1)]))
            self.act(st(4, [(1, 1)]), st(0, [(1, 1)]), AF.Ln, scale=1.0 / 384, bias=eps)
            self.act(st(5, [(1, 1)]), st(1, [(1, 1)]), AF.Ln, scale=1.0 / 256, bias=eps)
            self.act(st(6, [(1, 1)]), st(2, [(1, 1)]), AF.Ln, scale=1.0 / 32, bias=eps)
            self.act(st(8, [(1, 3)]), st(4, [(1, 3)]), AF.Exp, scale=-0.5)
            self.act(cqb(0, [(1, 384)]), self.bank(0)(0, [(1, 384)]), AF.Copy, scale=st(8, [(1, 1)]))
            self.act(cqb(384, [(1, 256)]), self.bank(1)(0, [(1, 256)]), AF.Copy, scale=st(9, [(1, 1)]))
            kpe = tmpk(0, [(1, 32)])
            self.stt(kpe, self.bank(1)(256, [(1, 32)]), st(10, [(1, 1)]), bvec(32, [(1, 32)]), ALU.mult, ALU.mult)

        def st2(t):
            cqb, cT, junk, sqq, sqk, qasm, kasm, tmpq, tmpk, st = bufs(t)
            for c in range(5):
                self.tr(self.bankb(2)(c * 128, [(1, 128)]), cqb(c * 128, [(1, 128)]))
            self.tt(cT(0, [(128, 5), (1, 128)]), self.bankb(2)(0, [(128, 5), (1, 128)]),
                    pvec(8, [(1, 5), (0, 128)]), ALU.mult)

        def st3(t):
            cqb, cT, junk, sqq, sqk, qasm, kasm, tmpq, tmpk, st = bufs(t)
            for (b, c0, n) in ((3, 0, 512), (4, 512, 256)):
                for c in range(3):
                    self.mm(self.bank(b)(0, [(1, n)]), cT(c * 128, [(1, 128)]), wq(c * 768 + c0, [(1, n)]),
                            start=(c == 0), stop=(c == 2))
            for (b, c0) in ((5, 0), (6, 512)):
                for c in range(2):
                    self.mm(self.bank(b)(0, [(1, 512)]), cT((3 + c) * 128, [(1, 128)]), wkv(c * 1024 + c0, [(1, 512)]),
                            start=(c == 0), stop=(c == 1))
            q_ps = self.bank(3)
            kv_ps = self.bank(5)
            self.act(sqq(0, [(1, 768)]), q_ps(0, [(1, 768)]), AF.Square)
            self.act(sqk(0, [(64, 8), (1, 64)]), kv_ps(0, [(128, 8), (1, 64)]), AF.Square)
            self.red(st(16, [(1, 24)]), sqq(0, [(32, 24), (1, 32)]))
            self.red(st(80, [(1, 8)]), sqk(0, [(64, 8), (1, 64)]))
            self.tt(st(40, [(1, 8)]), st(16, [(3, 8)]), st(17, [(3, 8)]), ALU.add)
            self.act(st(48, [(1, 8)]), st(40, [(1, 8)]), AF.Ln, scale=1.0 / 64, bias=eps)
            self.act(st(56, [(1, 8)]), st(18, [(3, 8)]), AF.Ln, scale=1.0 / 32, bias=eps)
            self.act(st(88, [(1, 8)]), st(80, [(1, 8)]), AF.Ln, scale=1.0 / 64, bias=eps)
            self.act(st(64, [(1, 16)]), st(48, [(1, 16)]), AF.Exp, scale=-0.5)
            self.act(st(96, [(1, 8)]), st(88, [(1, 8)]), AF.Exp, scale=-0.5)
            self.tt(qasm(0, [(96, 8), (1, 64)]), q_ps(0, [(96, 8), (1, 64)]), st(64, [(1, 8), (0, 64)]), ALU.mult)
            self.tt(kasm(0, [(96, 8), (1, 64)]), kv_ps(0, [(128, 8), (1, 64)]), st(96, [(1, 8), (0, 64)]), ALU.mult)
            qpe = tmpq(0, [(32, 8), (1, 32)])
            self.tt(qpe, q_ps(64, [(96, 8), (1, 32)]), st(72, [(1, 8), (0, 32)]), ALU.mult)
            self.rope(qasm(64, [(96, 8), (1, 32)]), tmpq, 8, 32, ropeAq, t * 64, t * 64 + 32, 256)
            self.rope(kasm(64, [(96, 8), (1, 32)]), tmpk, 1, 32, ropeA, t * 64, t * 64 + 32, 32, bcast=8)
            self.cp(vA(t * 8 * 66, [(66, 8), (1, 64)]), kv_ps(64, [(128, 8), (1, 64)]), eng="act")

        def st4(t):
            tb = t * 128
            cqb, cT, junk, sqq, sqk, qasm, kasm, tmpq, tmpk, st = bufs(t)
            for h in range(8):
                self.tr(self.bankb(2)(h * 128, [(1, 128)], np_=96), qasm(h * 96, [(1, 96)]))
            self.ts(qT(tb, [(2048, 8), (1, 128)], np_=96), self.bankb(2)(0, [(128, 8), (1, 128)], np_=96),
                    pvec(109, [(1, 1)], np_=96), ALU.mult)
            for h in range(8):
                self.tr(self.bankb(7)(h * 128, [(1, 128)], np_=96), kasm(h * 96, [(1, 96)]))
            self.ts(kT(tb, [(2048, 8), (1, 128)], np_=96), self.bankb(7)(0, [(128, 8), (1, 128)], np_=96),
                    pvec(110, [(1, 1)], np_=96), ALU.mult)

        stages = [p0, st1, st2, st3, st4]
        ns = len(stages)
        for i in range(NT + ns - 1):
            for s in range(ns - 1, -1, -1):
                t = i - s
                if 0 <= t < NT:
                    stages[s](t)

        if STOP == "p0":
            self.dump("hT", hT(0, [(1, 16384)]), [128, 16384])
            return
        if "qT" in DUMPS:
            self.dump("qT", qT(0, [(1, 16384)], np_=96), [96, 16384])
            self.dump("kT", kT(0, [(1, 16384)], np_=96), [96, 16384])
            self.dump("vA", vA(0, [(1, 16 * 8 * 66)]), [128, 16 * 8 * 66])
        if STOP == "p1a_prep":
            return

        self.wb_next = set(range(8))
        for c in range(SKIP_A, 4):
            for h in range(8):
                ob = 4 + (h % 2)
                self.attn_unit(c, lambda j, n, h=h: kT(h * 2048 + j * 128, [(1, 128)], np_=96),
                               lambda c0, n, h=h: qT(h * 2048 + c0, [(1, n)], np_=96),
                               lambda j, h=h: vA((j * 8 + h) * 66, [(1, 65)]),
                               [self.bank(ob)], 65, 4, PT, MLA_SCALE)
                o_ps = self.bank(ob)
                self.recip(rinv(h * 4, [(1, 4)]), o_ps(64, [(65, 4)]))
                self.tt(ocat(h * 64, [(512, 4), (1, 64)]), o_ps(0, [(65, 4), (1, 64)]),
                        rinv(h * 4, [(1, 4), (0, 64)]), ALU.mult)
            for il in range(4):
                for fc in range(4):
                    self.tr(self.bankb(6 + il % 2)(fc * 128, [(1, 128)]), ocat(il * 512 + fc * 128, [(1, 128)]))
                self.cp(oaT((c * 4 + il) * 128, [(2048, 4), (1, 128)]),
                        self.bankb(6 + il % 2)(0, [(128, 4), (1, 128)]), eng="dve")
        if "oaT" in DUMPS:
            self.dump("oaT", oaT(0, [(1, 8192)]), [128, 8192])

    def rope(self, dst, tmp, nh, dim, rbuf, cos_off, sin_off, tstride, bcast=None, eng="dve"):
        half = dim // 2
        x = tmp(0, [(dim, nh), (1, dim)])
        xsw = tmp(half, [(dim, nh), (-half, 2), (1, half)])
        t1s = tmp(tstride, [(dim, nh), (half, 2), (1, half)])
        t1 = tmp(tstride, [(dim, nh), (1, dim)])
        t2 = tmp(2 * tstride, [(dim, nh), (1, dim)])
        cosv = rbuf(cos_off, [(0, nh), (1, dim)])
        sins = rbuf(sin_off, [(0, nh), (half, 2), (1, half)])
        self.tt(t1s, xsw, sins, ALU.mult, eng=eng)
        self.tt(t2, x, cosv, ALU.mult, eng=eng)
        if bcast is None:
            self.tt(dst, t1, t2, ALU.add, eng=eng)
        else:
            self.tt(dst, tmp(tstride, [(0, bcast), (1, dim)]), tmp(2 * tstride, [(0, bcast), (1, dim)]), ALU.add, eng=eng)

    def attn_unit(self, c, kT_of, qT_of, v_of, obanks, vw, qpb, PT, scale):
        nk = 4 * c + 4
        npairs = nk // 2
        started = [False] * len(obanks)
        sb = [self.bank(0), self.bank(2)]

        def scores(p):
            sbank = sb[p % 2]
            for jj in range(2):
                j = 2 * p + jj
                i0 = max(4 * c, j)
                c0 = (i0 - 4 * c) * 128
                n = 512 - c0
                diag = j >= 4 * c
                self.mm(sbank(jj * 512 + c0, [(1, n)]), kT_of(j, 128), qT_of(c * 512 + c0, n),
                        start=True, stop=not diag, skip=True)
                if diag:
                    self.mm(sbank(jj * 512 + c0, [(1, 128)]), self.ident_ap, self.mask_ap,
                            start=False, stop=True, skip=True)

        def expo(p):
            sbank = sb[p % 2]
            pt = PT[p % 2]
            if 2 * p + 1 < 4 * c:
                self.act(pt(0, [(1, 1024)]), sbank(0, [(1, 1024)]), AF.Exp, scale=scale)
            else:
                for jj in range(2):
                    j = 2 * p + jj
                    c0 = (max(4 * c, j) - 4 * c) * 128
                    self.act(pt(jj * 512 + c0, [(1, 512 - c0)]), sbank(jj * 512 + c0, [(1, 512 - c0)]),
                             AF.Exp, scale=scale)

        def pv(p):
            pt = PT[p % 2]
            for jj in range(2):
                j = 2 * p + jj
                for i in range(max(4 * c, j), 4 * c + 4):
                    il = i - 4 * c
                    bi = il // qpb
                    ol = il % qpb
                    self.mm(obanks[bi](ol * vw, [(1, vw)]), pt(jj * 512 + il * 128, [(1, 128)]), v_of(j),
                            start=not started[bi], stop=(j == i), skip=True)
                    started[bi] = True

        scores(0)
        for p in range(npairs):
            expo(p)
            if p + 1 < npairs:
                scores(p + 1)
            pv(p)

    def diff_unit(self, c, h, dqT, dkT, vB, PT):
        nk = 4 * c + 4
        started = {}

        def scores(j):
            s0 = 2 * (j % 2)
            c0 = (max(4 * c, j) - 4 * c) * 128
            n = 512 - c0
            diag = j >= 4 * c
            for m in range(2):
                self.mm(self.bank(s0 + m)(c0, [(1, n)]),
                        dkT(h * 2048 + j * 128, [(1, 128)], p0=64 * m, np_=64),
                        dqT(h * 2048 + c * 512 + c0, [(1, n)], p0=64 * m, np_=64),
                        start=True, stop=not diag, skip=True)
            if diag:
                for m in range(2):
                    self.mm(self.bank(s0 + m)(c0, [(1, 128)]), self.ident_ap, self.mask_ap,
                            start=False, stop=True, skip=True)

        def expo(j):
            s0 = 2 * (j % 2)
            c0 = (max(4 * c, j) - 4 * c) * 128
            n = 512 - c0
            self.act(PT[j % 2](c0, [(512, 2), (1, n)]), self.bank(s0)(c0, [(512, 2), (1, n)]), AF.Exp,
                     scale=DIFF_SCALE)

        def pv(j):
            pt = PT[j % 2]
            for m in range(2):
                for i in range(max(4 * c, j), 4 * c + 4):
                    il = i - 4 * c
                    bk = 4 + 2 * m + il // 2
                    self.mm(self.bank(bk)((il % 2) * 129, [(1, 129)]), pt(m * 512 + il * 128, [(1, 128)]),
                            vB((j * 4 + h) * 130, [(1, 129)]),
                            start=bk not in started, stop=(j == i), skip=True)
                    started[bk] = True

        scores(0)
        for j in range(nk):
            expo(j)
            if j + 1 < nk:
                scores(j + 1)
            pv(j)

    def phase1b(self, R_X, R_H, R_OA, R_P):
        KB = 1024
        hT, pvec, bvec = self.hT, self.pvec, self.bvec
        oaT = self.bf(R_OA)
        win = self.bf(R_X)
        ropeB = self.f32(R_X + 24 * KB)
        xres = self.f32(R_X)
        wout = self.bf(R_H)
        self.xres = xres
        o = R_P
        dqT = self.bf(o); o += 16 * KB
        dkT = self.bf(o); o += 16 * KB
        vB = self.bf(o); o += 16 * 4 * 130 * 2
        tbase = o
        T = []
        for k in range(2):
            d = {}
            d["sq"] = self.f32(o); o += 512 * 4
            d["tmpr"] = self.f32(o); o += 3 * 512 * 4
            d["dasm"] = self.bf(o); o += 1024 * 2
            d["st"] = self.f32(o); o += 64 * 4
            T.append(d)
        assert o <= 188 * KB, o
        o2 = tbase
        PT = [self.bf(o2), self.bf(o2 + 2 * KB)]; o2 += 4 * KB
        ocat = self.bf(o2); o2 += 4 * 512 * 2
        obT = self.bf(o2); o2 += 4 * 512 * 2
        obs = [self.f32(o2), self.f32(o2 + 512 * 4)]; o2 += 2 * 512 * 4
        tmpo = self.f32(o2); o2 += 512 * 4
        rinv = self.f32(o2); o2 += 64
        stbs = [self.f32(o2), self.f32(o2 + 64)]; o2 += 128
        ocp = self.f32(o2); o2 += 4 * 258 * 4
        assert o2 <= 188 * KB, o2

        self.wb_next = set(range(8))
        self.load_w(win, self.win_d, 0, 8, 672, 1536, INC, 1536)
        self.dma(ropeB(0, [(128, 16), (1, 128)]), self.dram(self.ropeB_d, 0, [(128, 128), (128 * 128, 16), (1, 128)]))
        ropeG = [self.f32(R_X + 32 * KB), self.f32(R_X + 40 * KB)]
        for g in range(2):
            gb = 64 + g * 64
            self.tt(ropeG[g](0, [(128, 16), (1, 64)]), ropeB(0, [(128, 16), (1, 64)]),
                    bvec(gb, [(0, 16), (1, 64)]), ALU.mult)
            self.tt(ropeG[g](64, [(128, 16), (32, 2), (1, 32)]), ropeB(64, [(128, 16), (32, 2), (1, 32)]),
                    bvec(gb + 32, [(0, 16), (-32, 2), (1, 32)]), ALU.mult)
        self.memset(vB(128, [(130, 64), (1, 2)]), 1.0)
        eps = self.cst(0, [(1, 1)])

        def tile1b(t):
            tb = t * 128
            d = T[t % 2]
            sq, tmpr, dasm, st = d["sq"], d["tmpr"], d["dasm"], d["st"]
            b0 = 4 * (t % 2) if DBL_1B else 0
            for g in range(3):
                for c in range(8):
                    self.mm(self.bank(b0 + g)(0, [(1, 512)]), hT(c * 2048 + tb, [(1, 128)]),
                            win(c * 1536 + g * 512, [(1, 512)]), start=(c == 0), stop=(c == 7))
            for g in range(2):
                src = self.bank(b0 + g)
                self.act(sq(0, [(1, 512)]), src(0, [(1, 512)]), AF.Square)
                self.red(st(g * 8, [(1, 8)]), sq(0, [(64, 8), (1, 64)]))
            self.act(st(16, [(1, 16)]), st(0, [(1, 16)]), AF.Ln, scale=1.0 / 64, bias=eps)
            self.act(st(32, [(1, 16)]), st(16, [(1, 16)]), AF.Exp, scale=-0.5)
            for g in range(2):
                src = self.bank(b0 + g)
                xx = tmpr(0, [(64, 8), (1, 64)])
                self.tt(xx, src(0, [(64, 8), (1, 64)]), st(32 + g * 8, [(1, 8), (0, 64)]), ALU.mult)
                self.rope(dasm(g * 512, [(64, 8), (1, 64)]), tmpr, 8, 64, ropeG[g], t * 128, t * 128 + 64, 512, eng=PREP_ENG)
            for i in range(8):
                self.tr(self.bankb(b0 + 3)(i * 128, [(1, 128)]), dasm(i * 128, [(1, 128)]))
            self.cp(dqT(tb, [(2048, 4), (1, 128)]), self.bankb(b0 + 3)(0, [(128, 4), (1, 128)]), eng="act")
            self.cp(dkT(tb, [(2048, 4), (1, 128)]), self.bankb(b0 + 3)(512, [(128, 4), (1, 128)]), eng="act")
            self.cp(vB(t * 4 * 130, [(130, 4), (1, 128)]), self.bank(b0 + 2)(0, [(128, 4), (1, 128)]), eng="act")


        if ILV_1B:
            L = [self.capture(tile1b, t) for t in range(NT)]
            SPLIT = 30
            self.interleave([L[0][:SPLIT]])
            for t in range(NT):
                nxt = L[t + 1][:SPLIT] if t + 1 < NT else []
                self.interleave([L[t][SPLIT:], nxt])
        else:
            for t in range(NT):
                tile1b(t)

        if "dqT" in DUMPS:
            self.dump("dqT", dqT(0, [(1, 8192)]), [128, 8192])
            self.dump("dkT", dkT(0, [(1, 8192)]), [128, 8192])
            self.dump("vB", vB(0, [(1, 16 * 4 * 130)]), [128, 16 * 4 * 130])
        if STOP == "p1b_prep":
            return

        self.load_w(wout, self.wout_d, 0, 8, 0, 1024, D, 1024)
        self.wb_next = set(range(8))

        for c in range(4):
            for il in range(4):
                t = c * 4 + il
                self.dma(xres(t * 1024, [(1, 1024)]), self.dram(self.x_d, t * 128 * D, [(D, 128), (1, D)]))
            def E1(h):
                ob, stb = obs[h % 2], stbs[h % 2]
                for k in (0, 2):
                    self.cp(ocp(k * 258, [(258, 2), (1, 258)]), self.bank(4 + k)(0, [(512, 2), (1, 258)]), eng="dve")
                for m in range(2):
                    for bi in range(2):
                        self.recip(rinv(m * 4 + bi * 2, [(1, 2)]), ocp((2 * m + bi) * 258 + 128, [(129, 2)]))
                self.ts(rinv(4, [(1, 4)]), rinv(4, [(1, 4)]), self.neg_lam, ALU.mult)
                for bi in range(2):
                    o1 = ocp(bi * 258, [(129, 2), (1, 128)])
                    o2 = ocp((2 + bi) * 258, [(129, 2), (1, 128)])
                    t_ = tmpo(bi * 256, [(128, 2), (1, 128)])
                    ob_ = ob(bi * 256, [(128, 2), (1, 128)])
                    self.tt(t_, o2, rinv(4 + bi * 2, [(1, 2), (0, 128)]), ALU.mult)
                    self.tt(ob_, o1, rinv(bi * 2, [(1, 2), (0, 128)]), ALU.mult)
                    self.tt(ob_, ob_, t_, ALU.add)
                self.tt(tmpo(0, [(1, 512)]), ob(0, [(1, 512)]), ob(0, [(1, 512)]), ALU.mult)
                self.red(stb(0, [(1, 4)]), tmpo(0, [(128, 4), (1, 128)]))

            def E2(h):
                ob, stb = obs[h % 2], stbs[h % 2]
                self.act(stb(4, [(1, 4)]), stb(0, [(1, 4)]), AF.Ln, scale=1.0 / 128, bias=eps)
                self.act(stb(8, [(1, 4)]), stb(4, [(1, 4)]), AF.Exp, scale=-0.5, bias=self.cst(1, [(1, 1)]))
                self.tt(ob(0, [(128, 4), (1, 128)]), ob(0, [(128, 4), (1, 128)]), stb(8, [(1, 4), (0, 128)]), ALU.mult)
                self.tt(ocat(h * 128, [(512, 4), (1, 128)]), ob(0, [(128, 4), (1, 128)]),
                        bvec(192, [(0, 4), (1, 128)]), ALU.mult)

            for h in range(4):
                self.diff_unit(c, h, dqT, dkT, vB, PT)
                E1(h)
                if h > 0:
                    E2(h - 1)
            E2(3)
            for il in range(4):
                for fc in range(4):
                    self.tr(self.bankb(il % 2)(fc * 128, [(1, 128)]), ocat(il * 512 + fc * 128, [(1, 128)]))
                self.cp(obT(il * 128, [(512, 4), (1, 128)]), self.bankb(il % 2)(0, [(128, 4), (1, 128)]), eng="dve")
            for il in range(4):
                t = c * 4 + il
                for half in range(2):
                    pb = self.bank(2 + half)
                    for fc in range(8):
                        lhsT = oaT(fc * 2048 + t * 128, [(1, 128)]) if fc < 4 else obT((fc - 4) * 512 + il * 128, [(1, 128)])
                        self.mm(pb(0, [(1, 512)]), lhsT, wout(fc * 1024 + half * 512, [(1, 512)]),
                                start=(fc == 0), stop=(fc == 7))
                    xr = xres(t * 1024 + half * 512, [(1, 512)])
                    self.tt(xr, pb(0, [(1, 512)]), xr, ALU.add)
        if "x1" in DUMPS:
            self.dump("x1", xres(0, [(1, 16384)]), [128, 16384])

    def phase2(self, R_X, R_H, R_OA):
        KB = 1024
        xres = self.f32(R_X)
        hT = self.hT
        pvec = self.pvec
        GF = 4
        groups = [(g * GF, min(GF, NF - g * GF)) for g in range((NF + GF - 1) // GF)]
        o = R_OA
        wg = [self.bf(o), self.bf(o + 8 * KB)]; o += 16 * KB
        wu = [self.bf(o), self.bf(o + 8 * KB)]; o += 16 * KB
        wd = [self.bf(o), self.bf(o + 8 * KB)]; o += 16 * KB
        actb = self.bf(o); o += GF * 2048 * 2
        t0 = [self.f32(o), self.f32(o + 2 * KB)]; o += 4 * KB
        sg = [self.f32(o), self.f32(o + 2 * KB)]; o += 4 * KB
        halo = self.f32(o); o += NF * 2 * 4
        xb = [self.bf(o), self.bf(o + 2 * KB)]; o += 4 * KB
        st = self.f32(o); o += 48 * 4
        assert o <= 188 * KB, o
        self.memset(halo(0, [(1, NF * 2)]), 0.0)

        def load_group(gi):
            f0, nf = groups[gi]
            b = gi % 2
            self.load_w(wg[b], self.wg_d, 0, 8, f0 * 128, nf * 128, DFF, 512)
            self.load_w(wu[b], self.wu_d, 0, 8, f0 * 128, nf * 128, DFF, 512)
            self.load_w(wd[b], self.wd_d, f0 * 128, nf, 0, 1024, D, 1024)

        load_group(0)
        self.wb_next = set(range(8))
        for t in range(NT):
            self.norm_tile(t, xres(t * 1024, [(1, 1024)]), hT, 13, xb[t % 2], st, 6 + t % 2)
        for gi, (f0, nf) in enumerate(groups):
            b = gi % 2
            if gi + 1 < len(groups):
                load_group(gi + 1)
            for tc in range(4):
                for j in range(nf):
                    f = f0 + j
                    k = (tc * nf + j) % 2
                    gb = self.bank(k)
                    ub = self.bank(2 + k)
                    for c in range(8):
                        self.mm(gb(0, [(1, 512)]), wg[b](c * 512 + j * 128, [(1, 128)]),
                                hT(c * 2048 + tc * 512, [(1, 512)]), start=(c == 0), stop=(c == 7))
                    for c in range(8):
                        self.mm(ub(0, [(1, 512)]), wu[b](c * 512 + j * 128, [(1, 128)]),
                                hT(c * 2048 + tc * 512, [(1, 512)]), start=(c == 0), stop=(c == 7))
                    a = t0[k]
                    w0 = pvec(21 + f, [(1, 1)])
                    w1 = pvec(43 + f, [(1, 1)])
                    w2 = pvec(65 + f, [(1, 1)])
                    bb = pvec(87 + f, [(1, 1)])
                    self.act(a(0, [(1, 512)]), gb(0, [(1, 512)]), AF.Identity, scale=w2, bias=bb)
                    self.stt(a(1, [(1, 511)]), gb(0, [(1, 511)]), w1, a(1, [(1, 511)]), ALU.mult, ALU.add)
                    self.stt(a(0, [(1, 1)]), halo(f * 2 + 1, [(1, 1)]), w1, a(0, [(1, 1)]), ALU.mult, ALU.add)
                    self.stt(a(2, [(1, 510)]), gb(0, [(1, 510)]), w0, a(2, [(1, 510)]), ALU.mult, ALU.add)
                    self.stt(a(0, [(1, 2)]), halo(f * 2, [(1, 2)]), w0, a(0, [(1, 2)]), ALU.mult, ALU.add)
                    self.cp(halo(f * 2, [(1, 2)]), gb(510, [(1, 2)]), eng="dve")
                    self.act(sg[k](0, [(1, 512)]), a(0, [(1, 512)]), AF.Silu)
                    self.tt(actb(j * 2048 + tc * 512, [(1, 512)]), ub(0, [(1, 512)]), sg[k](0, [(1, 512)]), ALU.mult)
            for t in range(NT):
                for half in range(2):
                    pb = self.bank(4 + (t * 2 + half) % 4)
                    for j in range(nf):
                        self.mm(pb(0, [(1, 512)]), actb(j * 2048 + t * 128, [(1, 128)]),
                                wd[b](j * 1024 + half * 512, [(1, 512)]), start=(j == 0), stop=(j == nf - 1))
                    xr = xres(t * 1024 + half * 512, [(1, 512)])
                    self.tt(xr, pb(0, [(1, 512)]), xr, ALU.add)
                if gi == len(groups) - 1:
                    self.dma(self.dram(self.y_d, t * 128 * D, [(D, 128), (1, D)]), xres(t * 1024, [(1, 1024)]))


def _rope_table(dim):
    inv = (1.0 / (np.float32(10000.0) ** (np.arange(0, dim, 2, dtype=np.float32) / np.float32(dim)))).astype(np.float32)
    ang = (np.arange(S, dtype=np.float32)[:, None] * inv[None, :]).astype(np.float32)
    c, s = np.cos(ang).astype(np.float32), np.sin(ang).astype(np.float32)
    return np.ascontiguousarray(np.concatenate([c, c, -s, s], axis=1))


def _layout_consts(inp):
    f = lambda k: np.asarray(inp[k], dtype=np.float32)[0]
    pvec = np.zeros((128, 112), np.float32)
    pvec[:, 0:8] = f("attn_norm_g").reshape(8, 128).T
    pvec[:, 8:11] = f("q_a_norm_g").reshape(3, 128).T
    pvec[:, 11:13] = f("kv_a_norm_g").reshape(2, 128).T
    pvec[:, 13:21] = f("ffn_norm_g").reshape(8, 128).T
    cw = f("conv_w")
    for j in range(3):
        pvec[:, 21 + 22 * j: 43 + 22 * j] = cw[j].reshape(22, 128).T
    pvec[:, 87:109] = f("conv_b").reshape(22, 128).T
    pvec[:, 109] = 1.0
    pvec[:64, 109] = f("mla_q_norm_g")[:64]
    pvec[:, 110] = 1.0
    pvec[:64, 110] = f("mla_k_norm_g")[:64]
    bv = np.zeros((576,), np.float32)
    bv[0:32] = f("mla_q_norm_g")[64:]
    bv[32:64] = f("mla_k_norm_g")[64:]
    bv[64:128] = f("diff_q_norm_g")
    bv[128:192] = f("diff_k_norm_g")
    bv[192:320] = f("diff_subln_g")
    bv[320:384] = f("lambda_q1")
    bv[384:448] = f("lambda_k1")
    bv[448:512] = f("lambda_q2")
    bv[512:576] = f("lambda_k2")
    bvec = np.ascontiguousarray(np.broadcast_to(bv[None, :], (128, 576)))
    return pvec, bvec


_CACHE = {}


def _get_nc():
    key = (STOP, tuple(DUMPS))
    if key not in _CACHE:
        b = Builder()
        nc = b.build()
        _CACHE[key] = (nc, b)
    return _CACHE[key]


def kernel(**inp):
    nc, b = _get_nc()
    x = np.asarray(inp["x"], dtype=np.float32)
    pvec, bvec = _layout_consts(inp)
    ident = np.eye(128, dtype=np.float32).astype(ml_dtypes.bfloat16)
    kk = np.arange(128)[:, None]
    qq = np.arange(128)[None, :]
    maskT = np.where(qq >= kk, 0.0, NEG).astype(np.float32).astype(ml_dtypes.bfloat16)
    shared = {
        "w_in": np.ascontiguousarray(np.asarray(inp["w_in"], np.float32)[0]),
        "w_q_up": np.ascontiguousarray(np.asarray(inp["w_q_up"], np.float32)[0]),
        "w_kv_up": np.ascontiguousarray(np.asarray(inp["w_kv_up"], np.float32)[0]),
        "w_out": np.ascontiguousarray(np.asarray(inp["w_out"], np.float32)[0]),
        "w_gate": np.ascontiguousarray(np.asarray(inp["w_gate"], np.float32)[0]),
        "w_up": np.ascontiguousarray(np.asarray(inp["w_up"], np.float32)[0]),
        "w_down": np.ascontiguousarray(np.asarray(inp["w_down"], np.float32)[0]),
        "pvec": pvec, "bvec": bvec, "ropeA": _rope_table(32), "ropeB": _rope_table(64),
        "ident": ident, "maskT": maskT,
    }
    in_maps = []
    for i in range(8):
        m = dict(shared)
        m["x"] = np.ascontiguousarray(x[i])
        in_maps.append(m)
    res = run_bass_kernel_spmd(nc, in_maps, core_ids=list(range(8)))
    kernel.last = res
    if STOP is not None:
        return res
    return np.stack([np.asarray(r["y"], dtype=np.float32) for r in res.results], axis=0)
```

```python
import math
import numpy as np
import ml_dtypes
import concourse.bass as bass
import concourse.mybir as mybir
from concourse.bass_utils import run_bass_kernel_spmd

F32 = mybir.dt.float32
BF16 = mybir.dt.bfloat16
ALU = mybir.AluOpType
AF = mybir.ActivationFunctionType
AX = mybir.AxisListType
ESZ = {F32: 4, BF16: 2}

S = 2048
D = 1024
NT = 16
INC = 2208
DFF = 2816
NF = 22
EPS = 1e-6
LAM_INIT = 0.8 - 0.6 * math.exp(-0.3 * 0)
MLA_SCALE = 96 ** -0.5
DIFF_SCALE = 64 ** -0.5
NEG = -30000.0

STOP = None
REORDER = False
PREP_ENG = "dve"
DBL_1B = True
ILV_1B = True
SKEW_1B = False
WBANK = True
SKIP_A = False
DUMPS = []


class _Rec:
    __slots__ = ("p0", "p1", "b0", "b1", "ops", "w", "eng", "dead", "dma")


class _Op:
    __slots__ = ("id", "eng", "fn", "deps", "dma", "need", "sig", "dsem", "dtgt", "dprev", "cost", "lat", "pos", "waits")


class Sched:
    BK = 2048
    HOP = 900.0
    HOP_SAME = 250.0
    WIN = {"pe": 1, "act": 48, "dve": 64, "pool": 24, "sp": 12}

    def __init__(self):
        self.ops = []
        self.buckets = {}
        self.rkey = {}

    @staticmethod
    def rect(ap):
        name = ap.tensor.name
        esz = ESZ[ap.dtype]
        apl = ap.ap
        off = ap.offset
        if "DRAM" in str(ap.space):
            lo = hi = 0
            for s, c in apl:
                if s >= 0:
                    hi += s * (c - 1)
                else:
                    lo += s * (c - 1)
            return (name, 0, 1, (off + lo) * esz, (off + hi + 1) * esz)
        ps, pn = apl[0]
        p0 = off // ps
        col = off % ps
        lo = hi = 0
        for s, c in apl[1:]:
            if s >= 0:
                hi += s * (c - 1)
            else:
                lo += s * (c - 1)
        return (name, p0, p0 + pn, (col + lo) * esz, (col + hi + 1) * esz)

    def add(self, eng, fn, reads, writes, dma=False, cost=100.0, lat=0.0, wrects=()):
        op = _Op()
        op.id = len(self.ops)
        op.eng = eng
        op.fn = fn
        op.dma = dma
        op.need = False
        op.sig = None
        op.dsem = None
        op.cost = cost
        op.lat = lat
        deps = {}
        ops = self.ops

        def note(rec, raw):
            for d in rec.ops:
                if d == op.id:
                    continue
                dop = ops[d]
                sem = True
                if not dma and not dop.dma and dop.eng == eng:
                    sem = eng != "pe"
                if sem or d not in deps:
                    deps[d] = sem or deps.get(d, False)

        rr = [self.rect(a) for a in reads]
        ww = [self.rect(a) for a in writes] + list(wrects)
        BK = self.BK
        for (name, p0, p1, b0, b1) in rr:
            for bk in range(b0 // BK, (b1 - 1) // BK + 1):
                lst = self.buckets.get((name, bk))
                if not lst:
                    continue
                for rec in lst:
                    if rec.w and not rec.dead and rec.p0 < p1 and p0 < rec.p1 and rec.b0 < b1 and b0 < rec.b1:
                        note(rec, True)
        for (name, p0, p1, b0, b1) in ww:
            for bk in range(b0 // BK, (b1 - 1) // BK + 1):
                lst = self.buckets.get((name, bk))
                if not lst:
                    continue
                alive = []
                for rec in lst:
                    if rec.dead:
                        continue
                    if rec.p0 < p1 and p0 < rec.p1 and rec.b0 < b1 and b0 < rec.b1:
                        note(rec, False)
                        if rec.p0 >= p0 and rec.p1 <= p1 and rec.b0 >= b0 and rec.b1 <= b1:
                            rec.dead = True
                            continue
                    alive.append(rec)
                self.buckets[(name, bk)] = alive
        for (name, p0, p1, b0, b1) in rr:
            key = (name, eng, dma, p0, p1, b0, b1)
            rec = self.rkey.get(key)
            if rec is not None and not rec.dead:
                if rec.ops[-1] != op.id:
                    rec.ops.append(op.id)
                continue
            rec = _Rec()
            rec.p0, rec.p1, rec.b0, rec.b1, rec.ops, rec.w, rec.eng, rec.dead, rec.dma = p0, p1, b0, b1, [op.id], False, eng, False, dma
            self.rkey[key] = rec
            for bk in range(b0 // BK, (b1 - 1) // BK + 1):
                self.buckets.setdefault((name, bk), []).append(rec)
        for (name, p0, p1, b0, b1) in ww:
            rec = _Rec()
            rec.p0, rec.p1, rec.b0, rec.b1, rec.ops, rec.w, rec.eng, rec.dead, rec.dma = p0, p1, b0, b1, [op.id], True, eng, False, dma
            for bk in range(b0 // BK, (b1 - 1) // BK + 1):
                self.buckets.setdefault((name, bk), []).append(rec)
        op.deps = deps
        self.ops.append(op)
        return op

    def schedule(self):
        ops = self.ops
        n = len(ops)
        engs = ["pe", "act", "dve", "pool", "sp"]
        queue = {e: [op.id for op in ops if op.eng == e] for e in engs}
        ptr = {e: 0 for e in engs}
        sched = [False] * n
        done = [0.0] * n
        pending = [len(op.deps) for op in ops]
        ready_t = [0.0] * n
        users = [[] for _ in range(n)]
        for op in ops:
            for d, sem in op.deps.items():
                users[d].append((op.id, sem))
        efree = {e: 0.0 for e in engs}
        order = {e: [] for e in engs}
        remaining = n
        HOP, HOPS = self.HOP, self.HOP_SAME
        while remaining:
            best = None
            for e in engs:
                q = queue[e]
                i = ptr[e]
                lq = len(q)
                while i < lq and sched[q[i]]:
                    i += 1
                ptr[e] = i
                cnt = 0
                j = i
                W = self.WIN[e]
                ef = efree[e]
                while j < lq and cnt < W:
                    oid = q[j]
                    j += 1
                    if sched[oid]:
                        continue
                    cnt += 1
                    if pending[oid]:
                        continue
                    rt = ready_t[oid]
                    st = rt if rt > ef else ef
                    score = st + 3.0 * (cnt - 1)
                    if best is None or score < best[0]:
                        best = (score, st, e, oid)
                    if st <= ef:
                        break
            score, st, e, oid = best
            op = ops[oid]
            sched[oid] = True
            efree[e] = st + op.cost
            dn = st + op.cost + op.lat
            done[oid] = dn
            for (u, sem) in users[oid]:
                pending[u] -= 1
                if sem:
                    t = dn + (HOPS if (ops[u].eng == e and not op.dma and not ops[u].dma) else HOP)
                    if t > ready_t[u]:
                        ready_t[u] = t
            op.pos = len(order[e])
            order[e].append(op)
            remaining -= 1
        self.model_time = max(done) if n else 0.0
        return order

    def emit(self, nc, stack, reorder=True):
        ops = self.ops
        engs = ["pe", "act", "dve", "pool", "sp"]
        if reorder:
            per = self.schedule()
        else:
            per = {e: [op for op in ops if op.eng == e] for e in engs}
            for e in engs:
                for i, op in enumerate(per[e]):
                    op.pos = i
        for op in ops:
            best = {}
            dmas = []
            for d, sem in op.deps.items():
                if not sem:
                    continue
                dop = ops[d]
                if dop.dma:
                    dmas.append(dop)
                elif dop.eng not in best or best[dop.eng].pos < dop.pos:
                    best[dop.eng] = dop
            op.waits = (dmas, list(best.values()))
            for dop in best.values():
                dop.need = True
        SEG = 1000
        esem = {}
        for e in engs:
            c = 0
            for op in per[e]:
                if not op.dma and op.need:
                    k = c // SEG
                    if (e, k) not in esem:
                        esem[(e, k)] = stack.enter_context(nc.semaphore("s_%s_%d" % (e, k)))
                    op.sig = (esem[(e, k)], c % SEG + 1)
                    c += 1
        NDS = {"sp": 24, "pool": 12, "act": 1, "pe": 1, "dve": 1}
        dsems = {e: [stack.enter_context(nc.semaphore("d_%s_%d" % (e, i))) for i in range(NDS[e])] for e in engs}
        dcnt = {e: [0] * NDS[e] for e in engs}
        for e in engs:
            k = 0
            for op in per[e]:
                if op.dma:
                    i = k % NDS[e]
                    k += 1
                    op.dsem = dsems[e][i]
                    op.dprev = dcnt[e][i]
                    dcnt[e][i] += 16
                    op.dtgt = dcnt[e][i]
        block = stack.enter_context(nc.Block())

        def run(e, eng):
            waited = {}

            def wait(sem, val):
                k = id(sem)
                if waited.get(k, 0) >= val:
                    return
                waited[k] = val
                eng.wait_ge(sem, val)

            for op in per[e]:
                dmas, comps = op.waits
                for dop in dmas:
                    wait(dop.dsem, dop.dtgt)
                for dop in comps:
                    wait(dop.sig[0], dop.sig[1])
                if op.dma and op.dprev > 0:
                    wait(op.dsem, op.dprev)
                inst = op.fn(eng)
                if op.dma:
                    inst.then_inc(op.dsem, 16)
                elif op.need:
                    inst.then_inc(op.sig[0], 1)
            for i, sem in enumerate(dsems[e]):
                if dcnt[e][i] > 0:
                    wait(sem, dcnt[e][i])

        @block.tensor
        def _(eng):
            run("pe", eng)

        @block.scalar
        def _(eng):
            run("act", eng)

        @block.vector
        def _(eng):
            run("dve", eng)

        @block.gpsimd
        def _(eng):
            run("pool", eng)

        @block.sync
        def _(eng):
            run("sp", eng)


class Buf:
    def __init__(self, h, rowlen, col0):
        self.h, self.rowlen, self.col0 = h, rowlen, col0

    def __call__(self, off=0, dims=((1, 1),), p0=0, np_=128):
        return bass.AP(self.h, p0 * self.rowlen + self.col0 + off, [[self.rowlen, np_]] + [list(d) for d in dims])


class Builder:
    SB_BYTES = 192 * 1024

    def __init__(self):
        self.nc = bass.Bass("TRN2", target_bir_lowering=False)
        self.s = Sched()
        self.dump_specs = []
        self._cap = None
        self.wb_next = set()

    def _add(self, *a, **kw):
        if self._cap is not None:
            self._cap.append((a, kw))
        else:
            self.s.add(*a, **kw)

    def capture(self, f, *args):
        old = self._cap
        self._cap = []
        f(*args)
        lst = self._cap
        self._cap = old
        return lst

    def interleave(self, lists):
        lists = [l for l in lists if l]
        idx = [0] * len(lists)
        while True:
            best = None
            for k, l in enumerate(lists):
                if idx[k] < len(l):
                    fr = idx[k] / len(l)
                    if best is None or fr < best[0]:
                        best = (fr, k)
            if best is None:
                break
            k = best[1]
            a, kw = lists[k][idx[k]]
            idx[k] += 1
            self._add(*a, **kw)

    @staticmethod
    def nfree(ap):
        n = 1
        for s, c in ap.ap[1:]:
            n *= c
        return n

    def bank_rect(self, out):
        name, p0, p1, b0, b1 = Sched.rect(out)
        bk = b0 // 2048
        return (name, 0, 128, bk * 2048, bk * 2048 + 2048)

    def mm(self, out, lhsT, rhs, start=True, stop=True, skip=False):
        n = self.nfree(rhs)
        wr = ()
        if start:
            br = self.bank_rect(out)
            bk = br[3] // 2048
            if WBANK == 2 or bk in self.wb_next:
                self.wb_next.discard(bk)
                wr = (br,)
        self._add("pe", lambda e: e.matmul(out, lhsT, rhs, start=start, stop=stop, skip_group_check=skip),
                   [lhsT, rhs], [out], cost=max(n, 64) * 0.42 + 35.0, lat=60.0, wrects=wr)

    def tr(self, out, in_, ident=None):
        ident = ident if ident is not None else self.ident_ap
        self._add("pe", lambda e: e.transpose(out, in_, ident), [in_, ident], [out], cost=75.0, lat=60.0,
                   wrects=(self.bank_rect(out),) if WBANK else ())

    def act(self, out, in_, func, scale=None, bias=None, accum=None):
        reads = [in_]
        kw = {}
        if scale is not None:
            kw["scale"] = scale
            if not isinstance(scale, (int, float)):
                reads.append(scale)
        if bias is not None:
            kw["bias"] = bias
            if not isinstance(bias, (int, float)):
                reads.append(bias)
        writes = [out]
        if accum is not None:
            kw["accum_out"] = accum
            writes.append(accum)
        cost = 190.0 + self.nfree(in_) * 0.73 + (95.0 if accum is not None else 0.0)
        self._add("act", lambda e: e.activation(out, in_, func, **kw), reads, writes, cost=cost, lat=50.0)

    def _ecost(self, eng, n):
        if eng == "pool":
            return 250.0 + n * 1.45
        return 75.0 + n * 1.05

    def tt(self, out, a, b, op, eng="dve"):
        self._add(eng, lambda e: e.tensor_tensor(out, a, b, op), [a, b], [out],
                   cost=self._ecost(eng, self.nfree(out)), lat=50.0)

    def ts(self, out, a, s1, op0, s2=None, op1=None, eng="dve"):
        reads = [a]
        if not isinstance(s1, (int, float)):
            reads.append(s1)
        if s2 is not None and not isinstance(s2, (int, float)):
            reads.append(s2)
        c = self._ecost(eng, self.nfree(out))
        if op1 is None:
            self._add(eng, lambda e: e.tensor_scalar(out, a, s1, None, op0), reads, [out], cost=c, lat=50.0)
        else:
            self._add(eng, lambda e: e.tensor_scalar(out, a, s1, s2, op0, op1), reads, [out], cost=c, lat=50.0)

    def stt(self, out, a, sc, b, op0, op1):
        reads = [a, b]
        if not isinstance(sc, (int, float)):
            reads.append(sc)
        self._add("dve", lambda e: e.scalar_tensor_tensor(out, a, sc, b, op0, op1), reads, [out],
                   cost=self._ecost("dve", self.nfree(out)), lat=50.0)

    def red(self, out, in_, op=ALU.add):
        self._add("dve", lambda e: e.tensor_reduce(out, in_, AX.X, op), [in_], [out],
                   cost=self._ecost("dve", self.nfree(in_)), lat=50.0)

    def recip(self, out, in_):
        self._add("dve", lambda e: e.reciprocal(out, in_), [in_], [out],
                   cost=75.0 + 8.4 * self.nfree(out), lat=50.0)

    def cp(self, out, in_, eng="dve"):
        if eng == "act":
            self._add("act", lambda e: e.copy(out, in_), [in_], [out],
                       cost=190.0 + self.nfree(in_) * 0.73, lat=50.0)
        else:
            self._add(eng, lambda e: e.tensor_copy(out, in_), [in_], [out],
                       cost=self._ecost(eng, self.nfree(out)), lat=50.0)

    def memset(self, ap, val, eng="pool"):
        self._add(eng, lambda e: e.memset(ap, val), [], [ap], cost=self._ecost(eng, self.nfree(ap)), lat=50.0)

    def dma(self, out, in_, q="sp", **kw):
        nbytes = self.nfree(out) * out.ap[0][1] * ESZ[out.dtype] if "DRAM" not in str(out.space) else \
            self.nfree(in_) * in_.ap[0][1] * ESZ[in_.dtype]
        if q == "sp":
            cost, lat = 70.0, 2200.0 + nbytes / 120.0
        else:
            cost, lat = 700.0, 2600.0 + nbytes / 90.0
        self._add(q, lambda e: e.dma_start(out, in_, **kw), [in_], [out], dma=True, cost=cost, lat=lat)

    def rsqrt(self, out, ss, tmp, inv_n, extra_bias=None):
        self.act(tmp, ss, AF.Ln, scale=inv_n, bias=self.cst(0, [(1, 1)]))
        if extra_bias is None:
            self.act(out, tmp, AF.Exp, scale=-0.5)
        else:
            self.act(out, tmp, AF.Exp, scale=-0.5, bias=extra_bias)

    def build(self):
        import contextlib
        nc = self.nc
        dt = nc.dram_tensor
        self.x_d = dt("x", [S, D], F32, kind="ExternalInput")
        self.win_d = dt("w_in", [D, INC], F32, kind="ExternalInput")
        self.wq_d = dt("w_q_up", [384, 768], F32, kind="ExternalInput")
        self.wkv_d = dt("w_kv_up", [256, 1024], F32, kind="ExternalInput")
        self.wout_d = dt("w_out", [D, D], F32, kind="ExternalInput")
        self.wg_d = dt("w_gate", [D, DFF], F32, kind="ExternalInput")
        self.wu_d = dt("w_up", [D, DFF], F32, kind="ExternalInput")
        self.wd_d = dt("w_down", [DFF, D], F32, kind="ExternalInput")
        self.pvec_d = dt("pvec", [128, 112], F32, kind="ExternalInput")
        self.bvec_d = dt("bvec", [128, 576], F32, kind="ExternalInput")
        self.ropeA_d = dt("ropeA", [S, 64], F32, kind="ExternalInput")
        self.ropeB_d = dt("ropeB", [S, 128], F32, kind="ExternalInput")
        self.ident_d = dt("ident", [128, 128], BF16, kind="ExternalInput")
        self.mask_d = dt("maskT", [128, 128], BF16, kind="ExternalInput")
        self.y_d = dt("y", [S, D], F32, kind="ExternalOutput")
        with contextlib.ExitStack() as stack:
            big = stack.enter_context(nc.sbuf_tensor("big", [128, self.SB_BYTES // 4], F32))
            ps = stack.enter_context(nc.psum_tensor("ps", [128, 4096], F32))
            self.big, self.ps = big, ps
            self.bigb = big.bitcast(BF16)
            self.psb = ps.bitcast(BF16)
            self.program()
            self.s.emit(nc, stack, reorder=REORDER)
        return nc

    def f32(self, byte_off):
        assert byte_off % 4 == 0
        return Buf(self.big, self.SB_BYTES // 4, byte_off // 4)

    def bf(self, byte_off):
        assert byte_off % 2 == 0
        return Buf(self.bigb, self.SB_BYTES // 2, byte_off // 2)

    def bank(self, b):
        return Buf(self.ps, 4096, b * 512)

    def bankb(self, b):
        return Buf(self.psb, 8192, b * 1024)

    def dram(self, h, off, dims):
        return bass.AP(h, off, [list(d) for d in dims])

    def dump(self, name, ap, shape):
        h = self.nc.dram_tensor("dbg_" + name, list(shape), ap.dtype, kind="ExternalOutput")
        self.dma(h.ap(), ap)
        self.dump_specs.append("dbg_" + name)

    def program(self):
        KB = 1024
        top = 188 * KB
        self.cst = self.f32(top)
        self.pvec = self.f32(top + 64)
        self.bvec = self.f32(top + 64 + 448)
        identb = self.bf(top + 64 + 448 + 2304)
        maskb = self.bf(top + 64 + 448 + 2304 + 256)
        lamb = self.f32(top + 64 + 448 + 2304 + 512)
        self.ident_ap = identb(0, [(1, 128)])
        self.mask_ap = maskb(0, [(1, 128)])
        cst, pvec, bvec = self.cst, self.pvec, self.bvec

        self.memset(cst(0, [(1, 1)]), EPS, eng="dve")
        self.memset(cst(1, [(1, 1)]), math.log(1.0 - LAM_INIT), eng="dve")
        self.dma(pvec(0, [(1, 112)]), self.pvec_d.ap())
        self.dma(bvec(0, [(1, 576)]), self.bvec_d.ap())
        self.dma(self.ident_ap, self.ident_d.ap())
        self.dma(self.mask_ap, self.mask_d.ap())

        ltmp = self.f32(176 * KB)
        self.tt(ltmp(0, [(1, 64)]), bvec(320, [(1, 64)]), bvec(384, [(1, 64)]), ALU.mult)
        self.red(lamb(0, [(1, 1)]), ltmp(0, [(1, 64)]))
        self.tt(ltmp(64, [(1, 64)]), bvec(448, [(1, 64)]), bvec(512, [(1, 64)]), ALU.mult)
        self.red(lamb(1, [(1, 1)]), ltmp(64, [(1, 64)]))
        self.act(lamb(2, [(1, 2)]), lamb(0, [(1, 2)]), AF.Exp)
        self.tt(lamb(4, [(1, 1)]), lamb(2, [(1, 1)]), lamb(3, [(1, 1)]), ALU.subtract)
        self.ts(lamb(5, [(1, 1)]), lamb(4, [(1, 1)]), -1.0, ALU.mult, -LAM_INIT, ALU.add)
        self.neg_lam = lamb(5, [(1, 1)])

        R_X = 0
        R_H = 64 * KB
        R_OA = 96 * KB
        R_P = 112 * KB
        self.hT = self.bf(R_H)

        self.phase1a(R_X, R_OA, R_P)
        if STOP in ("p0", "p1a", "p1a_prep"):
            return
        self.phase1b(R_X, R_H, R_OA, R_P)
        if STOP in ("p1b", "p1b_prep"):
            return
        self.phase2(R_X, R_H, R_OA)

    def norm_tile(self, t, xa, hT, gcol, xb, st, pb, load=None):
        if load is not None:
            self.dma(xa, load)
        self.act(xb(0, [(1, 1024)]), xa, AF.Square, accum=st(t, [(1, 1)]))
        self.rsqrt(st(32 + t, [(1, 1)]), st(t, [(1, 1)]), st(16 + t, [(1, 1)]), 1.0 / D)
        self.act(xb(0, [(1, 1024)]), xa, AF.Copy, scale=st(32 + t, [(1, 1)]))
        for c in range(8):
            self.tr(self.bankb(pb)(c * 128, [(1, 128)]), xb(c * 128, [(1, 128)]))
        self.tt(hT(t * 128, [(2048, 8), (1, 128)]), self.bankb(pb)(0, [(128, 8), (1, 128)]),
                self.pvec(gcol, [(1, 8), (0, 128)]), ALU.mult)

    def load_w(self, dst, src_h, row0, nrows_chunks, col0, ncols, ld, dst_cstride):
        for c in range(nrows_chunks):
            c0 = 0
            while c0 < ncols:
                n = min(1024, ncols - c0)
                self.dma(dst(c * dst_cstride + c0, [(1, n)]),
                         self.dram(src_h, (row0 + c * 128) * ld + col0 + c0, [(ld, 128), (1, n)]), q="pool")
                c0 += n

    def phase1a(self, R_X, R_OA, R_P):
        KB = 1024
        hT, pvec, bvec = self.hT, self.pvec, self.bvec
        qT = self.bf(R_X)
        kT = self.bf(R_X + 32 * KB)
        oaT = self.bf(R_OA)
        xs = [self.f32(R_OA), self.f32(R_OA + 4 * KB)]
        xb0 = [self.bf(R_OA + 8 * KB), self.bf(R_OA + 10 * KB)]
        st0 = self.f32(R_OA + 12 * KB)
        o = R_P
        win = self.bf(o); o += 8 * 672 * 2
        wq = self.bf(o); o += 3 * 768 * 2
        wkv = self.bf(o); o += 2 * 1024 * 2
        vA = self.bf(o); o += 16 * 8 * 66 * 2
        ropeA = self.f32(o); o += 16 * 64 * 4
        tbase = o
        T = []
        for k in range(2):
            d = {}
            d["cqb"] = self.bf(o); o += 640 * 2
            d["cT"] = self.bf(o); o += 640 * 2
            d["junk"] = self.bf(o); o += 672 * 2
            d["sqq"] = self.f32(o); o += 768 * 4
            d["sqk"] = self.f32(o); o += 512 * 4
            d["qasm"] = self.bf(o); o += 768 * 2
            d["kasm"] = self.bf(o); o += 768 * 2
            d["tmpq"] = self.f32(o); o += 768 * 4
            d["tmpk"] = self.f32(o); o += 96 * 4
            d["st"] = self.f32(o); o += 128 * 4
            T.append(d)
        ropeAq = self.f32(o); o += 16 * 64 * 4
        assert o <= 188 * KB, o
        o2 = tbase
        PT = [self.bf(o2), self.bf(o2 + 2 * KB)]; o2 += 4 * KB
        ocat = self.bf(o2); o2 += 4 * 512 * 2
        rinv = self.f32(o2); o2 += 128
        self.vA = vA

        self.load_w(win, self.win_d, 0, 8, 0, 672, INC, 672)
        self.load_w(wq, self.wq_d, 0, 3, 0, 768, 768, 768)
        self.load_w(wkv, self.wkv_d, 0, 2, 0, 1024, 1024, 1024)
        self.dma(ropeA(0, [(64, 16), (1, 64)]), self.dram(self.ropeA_d, 0, [(64, 128), (128 * 64, 16), (1, 64)]))
        self.tt(ropeAq(0, [(64, 16), (1, 32)]), ropeA(0, [(64, 16), (1, 32)]), bvec(0, [(0, 16), (1, 32)]), ALU.mult)
        self.tt(ropeAq(32, [(64, 16), (16, 2), (1, 16)]), ropeA(32, [(64, 16), (16, 2), (1, 16)]),
                bvec(16, [(0, 16), (-16, 2), (1, 16)]), ALU.mult)
        self.memset(vA(64, [(66, 128), (1, 2)]), 1.0)

        def p0(t):
            self.norm_tile(t, xs[t % 2](0, [(1, 1024)]), hT, 0, xb0[t % 2], st0, 7,
                           load=self.dram(self.x_d, t * 128 * D, [(D, 128), (1, D)]))

        eps = self.cst(0, [(1, 1)])

        def bufs(t):
            d = T[t % 2]
            return tuple(d[k] for k in ("cqb", "cT", "junk", "sqq", "sqk", "qasm", "kasm", "tmpq", "tmpk", "st"))

        def st1(t):
            tb = t * 128
            cqb, cT, junk, sqq, sqk, qasm, kasm, tmpq, tmpk, st = bufs(t)
            for c in range(8):
                self.mm(self.bank(0)(0, [(1, 384)]), hT(c * 2048 + tb, [(1, 128)]), win(c * 672, [(1, 384)]),
                        start=(c == 0), stop=(c == 7))
            for c in range(8):
                self.mm(self.bank(1)(0, [(1, 288)]), hT(c * 2048 + tb, [(1, 128)]), win(c * 672 + 384, [(1, 288)]),
                        start=(c == 0), stop=(c == 7))
            self.act(junk(0, [(1, 384)]), self.bank(0)(0, [(1, 384)]), AF.Square, accum=st(0, [(1, 1)]))
            self.act(junk(384, [(1, 256)]), self.bank(1)(0, [(1, 256)]), AF.Square, accum=st(1, [(1, 1)]))
            self.act(junk(640, [(1, 32)]), self.bank(1)(256, [(1, 32)]), AF.Square, accum=st(2, [(1, 1)]))
            self.act(st(4, [(1, 1)]), st(0, [(1, 1)]), AF.Ln, scale=1.0 / 384, bias=eps)
            self.act(st(5, [(1, 1)]), st(1, [(1, 1)]), AF.Ln, scale=1.0 / 256, bias=eps)
            self.act(st(6, [(1, 1)]), st(2, [(1, 1)]), AF.Ln, scale=1.0 / 32, bias=eps)
            self.act(st(8, [(1, 3)]), st(4, [(1, 3)]), AF.Exp, scale=-0.5)
            self.act(cqb(0, [(1, 384)]), self.bank(0)(0, [(1, 384)]), AF.Copy, scale=st(8, [(1, 1)]))
            self.act(cqb(384, [(1, 256)]), self.bank(1)(0, [(1, 256)]), AF.Copy, scale=st(9, [(1, 1)]))
            kpe = tmpk(0, [(1, 32)])
            self.stt(kpe, self.bank(1)(256, [(1, 32)]), st(10, [(1, 1)]), bvec(32, [(1, 32)]), ALU.mult, ALU.mult)

        def st2(t):
            cqb, cT, junk, sqq, sqk, qasm, kasm, tmpq, tmpk, st = bufs(t)
            for c in range(5):
                self.tr(self.bankb(2)(c * 128, [(1, 128)]), cqb(c * 128, [(1, 128)]))
            self.tt(cT(0, [(128, 5), (1, 128)]), self.bankb(2)(0, [(128, 5), (1, 128)]),
                    pvec(8, [(1, 5), (0, 128)]), ALU.mult)

        def st3(t):
            cqb, cT, junk, sqq, sqk, qasm, kasm, tmpq, tmpk, st = bufs(t)
            for (b, c0, n) in ((3, 0, 512), (4, 512, 256)):
                for c in range(3):
                    self.mm(self.bank(b)(0, [(1, n)]), cT(c * 128, [(1, 128)]), wq(c * 768 + c0, [(1, n)]),
                            start=(c == 0), stop=(c == 2))
            for (b, c0) in ((5, 0), (6, 512)):
                for c in range(2):
                    self.mm(self.bank(b)(0, [(1, 512)]), cT((3 + c) * 128, [(1, 128)]), wkv(c * 1024 + c0, [(1, 512)]),
                            start=(c == 0), stop=(c == 1))
            q_ps = self.bank(3)
            kv_ps = self.bank(5)
            self.act(sqq(0, [(1, 768)]), q_ps(0, [(1, 768)]), AF.Square)
            self.act(sqk(0, [(64, 8), (1, 64)]), kv_ps(0, [(128, 8), (1, 64)]), AF.Square)
            self.red(st(16, [(1, 24)]), sqq(0, [(32, 24), (1, 32)]))
            self.red(st(80, [(1, 8)]), sqk(0, [(64, 8), (1, 64)]))
            self.tt(st(40, [(1, 8)]), st(16, [(3, 8)]), st(17, [(3, 8)]), ALU.add)
            self.act(st(48, [(1, 8)]), st(40, [(1, 8)]), AF.Ln, scale=1.0 / 64, bias=eps)
            self.act(st(56, [(1, 8)]), st(18, [(3, 8)]), AF.Ln, scale=1.0 / 32, bias=eps)
            self.act(st(88, [(1, 8)]), st(80, [(1, 8)]), AF.Ln, scale=1.0 / 64, bias=eps)
            self.act(st(64, [(1, 16)]), st(48, [(1, 16)]), AF.Exp, scale=-0.5)
            self.act(st(96, [(1, 8)]), st(88, [(1, 8)]), AF.Exp, scale=-0.5)
            self.tt(qasm(0, [(96, 8), (1, 64)]), q_ps(0, [(96, 8), (1, 64)]), st(64, [(1, 8), (0, 64)]), ALU.mult)
            self.tt(kasm(0, [(96, 8), (1, 64)]), kv_ps(0, [(128, 8), (1, 64)]), st(96, [(1, 8), (0, 64)]), ALU.mult)
            qpe = tmpq(0, [(32, 8), (1, 32)])
            self.tt(qpe, q_ps(64, [(96, 8), (1, 32)]), st(72, [(1, 8), (0, 32)]), ALU.mult)
            self.rope(qasm(64, [(96, 8), (1, 32)]), tmpq, 8, 32, ropeAq, t * 64, t * 64 + 32, 256)
            self.rope(kasm(64, [(96, 8), (1, 32)]), tmpk, 1, 32, ropeA, t * 64, t * 64 + 32, 32, bcast=8)
            self.cp(vA(t * 8 * 66, [(66, 8), (1, 64)]), kv_ps(64, [(128, 8), (1, 64)]), eng="act")

        def st4(t):
            tb = t * 128
            cqb, cT, junk, sqq, sqk, qasm, kasm, tmpq, tmpk, st = bufs(t)
            for h in range(8):
                self.tr(self.bankb(2)(h * 128, [(1, 128)], np_=96), qasm(h * 96, [(1, 96)]))
            self.ts(qT(tb, [(2048, 8), (1, 128)], np_=96), self.bankb(2)(0, [(128, 8), (1, 128)], np_=96),
                    pvec(109, [(1, 1)], np_=96), ALU.mult)
            for h in range(8):
                self.tr(self.bankb(7)(h * 128, [(1, 128)], np_=96), kasm(h * 96, [(1, 96)]))
            self.ts(kT(tb, [(2048, 8), (1, 128)], np_=96), self.bankb(7)(0, [(128, 8), (1, 128)], np_=96),
                    pvec(110, [(1, 1)], np_=96), ALU.mult)

        stages = [p0, st1, st2, st3, st4]
        ns = len(stages)
        for i in range(NT + ns - 1):
            for s in range(ns - 1, -1, -1):
                t = i - s
                if 0 <= t < NT:
                    stages[s](t)

        if STOP == "p0":
            self.dump("hT", hT(0, [(1, 16384)]), [128, 16384])
            return
        if "qT" in DUMPS:
            self.dump("qT", qT(0, [(1, 16384)], np_=96), [96, 16384])
            self.dump("kT", kT(0, [(1, 16384)], np_=96), [96, 16384])
            self.dump("vA", vA(0, [(1, 16 * 8 * 66)]), [128, 16 * 8 * 66])
        if STOP == "p1a_prep":
            return

        self.wb_next = set(range(8))
        for c in range(SKIP_A, 4):
            for h in range(8):
                ob = 4 + (h % 2)
                self.attn_unit(c, lambda j, n, h=h: kT(h * 2048 + j * 128, [(1, 128)], np_=96),
                               lambda c0, n, h=h: qT(h * 2048 + c0, [(1, n)], np_=96),
                               lambda j, h=h: vA((j * 8 + h) * 66, [(1, 65)]),
                               [self.bank(ob)], 65, 4, PT, MLA_SCALE)
                o_ps = self.bank(ob)
                self.recip(rinv(h * 4, [(1, 4)]), o_ps(64, [(65, 4)]))
                self.tt(ocat(h * 64, [(512, 4), (1, 64)]), o_ps(0, [(65, 4), (1, 64)]),
                        rinv(h * 4, [(1, 4), (0, 64)]), ALU.mult)
            for il in range(4):
                for fc in range(4):
                    self.tr(self.bankb(6 + il % 2)(fc * 128, [(1, 128)]), ocat(il * 512 + fc * 128, [(1, 128)]))
                self.cp(oaT((c * 4 + il) * 128, [(2048, 4), (1, 128)]),
                        self.bankb(6 + il % 2)(0, [(128, 4), (1, 128)]), eng="dve")
        if "oaT" in DUMPS:
            self.dump("oaT", oaT(0, [(1, 8192)]), [128, 8192])

    def rope(self, dst, tmp, nh, dim, rbuf, cos_off, sin_off, tstride, bcast=None, eng="dve"):
        half = dim // 2
        x = tmp(0, [(dim, nh), (1, dim)])
        xsw = tmp(half, [(dim, nh), (-half, 2), (1, half)])
        t1s = tmp(tstride, [(dim, nh), (half, 2), (1, half)])
        t1 = tmp(tstride, [(dim, nh), (1, dim)])
        t2 = tmp(2 * tstride, [(dim, nh), (1, dim)])
        cosv = rbuf(cos_off, [(0, nh), (1, dim)])
        sins = rbuf(sin_off, [(0, nh), (half, 2), (1, half)])
        self.tt(t1s, xsw, sins, ALU.mult, eng=eng)
        self.tt(t2, x, cosv, ALU.mult, eng=eng)
        if bcast is None:
            self.tt(dst, t1, t2, ALU.add, eng=eng)
        else:
            self.tt(dst, tmp(tstride, [(0, bcast), (1, dim)]), tmp(2 * tstride, [(0, bcast), (1, dim)]), ALU.add, eng=eng)

    def attn_unit(self, c, kT_of, qT_of, v_of, obanks, vw, qpb, PT, scale):
        nk = 4 * c + 4
        npairs = nk // 2
        started = [False] * len(obanks)
        sb = [self.bank(0), self.bank(2)]

        def scores(p):
            sbank = sb[p % 2]
            for jj in range(2):
                j = 2 * p + jj
                i0 = max(4 * c, j)
                c0 = (i0 - 4 * c) * 128
                n = 512 - c0
                diag = j >= 4 * c
                self.mm(sbank(jj * 512 + c0, [(1, n)]), kT_of(j, 128), qT_of(c * 512 + c0, n),
                        start=True, stop=not diag, skip=True)
                if diag:
                    self.mm(sbank(jj * 512 + c0, [(1, 128)]), self.ident_ap, self.mask_ap,
                            start=False, stop=True, skip=True)

        def expo(p):
            sbank = sb[p % 2]
            pt = PT[p % 2]
            if 2 * p + 1 < 4 * c:
                self.act(pt(0, [(1, 1024)]), sbank(0, [(1, 1024)]), AF.Exp, scale=scale)
            else:
                for jj in range(2):
                    j = 2 * p + jj
                    c0 = (max(4 * c, j) - 4 * c) * 128
                    self.act(pt(jj * 512 + c0, [(1, 512 - c0)]), sbank(jj * 512 + c0, [(1, 512 - c0)]),
                             AF.Exp, scale=scale)

        def pv(p):
            pt = PT[p % 2]
            for jj in range(2):
                j = 2 * p + jj
                for i in range(max(4 * c, j), 4 * c + 4):
                    il = i - 4 * c
                    bi = il // qpb
                    ol = il % qpb
                    self.mm(obanks[bi](ol * vw, [(1, vw)]), pt(jj * 512 + il * 128, [(1, 128)]), v_of(j),
                            start=not started[bi], stop=(j == i), skip=True)
                    started[bi] = True

        scores(0)
        for p in range(npairs):
            expo(p)
            if p + 1 < npairs:
                scores(p + 1)
            pv(p)

    def diff_unit(self, c, h, dqT, dkT, vB, PT):
        nk = 4 * c + 4
        started = {}

        def scores(j):
            s0 = 2 * (j % 2)
            c0 = (max(4 * c, j) - 4 * c) * 128
            n = 512 - c0
            diag = j >= 4 * c
            for m in range(2):
                self.mm(self.bank(s0 + m)(c0, [(1, n)]),
                        dkT(h * 2048 + j * 128, [(1, 128)], p0=64 * m, np_=64),
                        dqT(h * 2048 + c * 512 + c0, [(1, n)], p0=64 * m, np_=64),
                        start=True, stop=not diag, skip=True)
            if diag:
                for m in range(2):
                    self.mm(self.bank(s0 + m)(c0, [(1, 128)]), self.ident_ap, self.mask_ap,
                            start=False, stop=True, skip=True)

        def expo(j):
            s0 = 2 * (j % 2)
            c0 = (max(4 * c, j) - 4 * c) * 128
            n = 512 - c0
            self.act(PT[j % 2](c0, [(512, 2), (1, n)]), self.bank(s0)(c0, [(512, 2), (1, n)]), AF.Exp,
                     scale=DIFF_SCALE)

        def pv(j):
            pt = PT[j % 2]
            for m in range(2):
                for i in range(max(4 * c, j), 4 * c + 4):
                    il = i - 4 * c
                    bk = 4 + 2 * m + il // 2
                    self.mm(self.bank(bk)((il % 2) * 129, [(1, 129)]), pt(m * 512 + il * 128, [(1, 128)]),
                            vB((j * 4 + h) * 130, [(1, 129)]),
                            start=bk not in started, stop=(j == i), skip=True)
                    started[bk] = True

        scores(0)
        for j in range(nk):
            expo(j)
            if j + 1 < nk:
                scores(j + 1)
            pv(j)

    def phase1b(self, R_X, R_H, R_OA, R_P):
        KB = 1024
        hT, pvec, bvec = self.hT, self.pvec, self.bvec
        oaT = self.bf(R_OA)
        win = self.bf(R_X)
        ropeB = self.f32(R_X + 24 * KB)
        xres = self.f32(R_X)
        wout = self.bf(R_H)
        self.xres = xres
        o = R_P
        dqT = self.bf(o); o += 16 * KB
        dkT = self.bf(o); o += 16 * KB
        vB = self.bf(o); o += 16 * 4 * 130 * 2
        tbase = o
        T = []
        for k in range(2):
            d = {}
            d["sq"] = self.f32(o); o += 512 * 4
            d["tmpr"] = self.f32(o); o += 3 * 512 * 4
            d["dasm"] = self.bf(o); o += 1024 * 2
            d["st"] = self.f32(o); o += 64 * 4
            T.append(d)
        assert o <= 188 * KB, o
        o2 = tbase
        PT = [self.bf(o2), self.bf(o2 + 2 * KB)]; o2 += 4 * KB
        ocat = self.bf(o2); o2 += 4 * 512 * 2
        obT = self.bf(o2); o2 += 4 * 512 * 2
        obs = [self.f32(o2), self.f32(o2 + 512 * 4)]; o2 += 2 * 512 * 4
        tmpo = self.f32(o2); o2 += 512 * 4
        rinv = self.f32(o2); o2 += 64
        stbs = [self.f32(o2), self.f32(o2 + 64)]; o2 += 128
        ocp = self.f32(o2); o2 += 4 * 258 * 4
        assert o2 <= 188 * KB, o2

        self.wb_next = set(range(8))
        self.load_w(win, self.win_d, 0, 8, 672, 1536, INC, 1536)
        self.dma(ropeB(0, [(128, 16), (1, 128)]), self.dram(self.ropeB_d, 0, [(128, 128), (128 * 128, 16), (1, 128)]))
        ropeG = [self.f32(R_X + 32 * KB), self.f32(R_X + 40 * KB)]
        for g in range(2):
            gb = 64 + g * 64
            self.tt(ropeG[g](0, [(128, 16), (1, 64)]), ropeB(0, [(128, 16), (1, 64)]),
                    bvec(gb, [(0, 16), (1, 64)]), ALU.mult)
            self.tt(ropeG[g](64, [(128, 16), (32, 2), (1, 32)]), ropeB(64, [(128, 16), (32, 2), (1, 32)]),
                    bvec(gb + 32, [(0, 16), (-32, 2), (1, 32)]), ALU.mult)
        self.memset(vB(128, [(130, 64), (1, 2)]), 1.0)
        eps = self.cst(0, [(1, 1)])

        def tile1b(t):
            tb = t * 128
            d = T[t % 2]
            sq, tmpr, dasm, st = d["sq"], d["tmpr"], d["dasm"], d["st"]
            b0 = 4 * (t % 2) if DBL_1B else 0
            for g in range(3):
                for c in range(8):
                    self.mm(self.bank(b0 + g)(0, [(1, 512)]), hT(c * 2048 + tb, [(1, 128)]),
                            win(c * 1536 + g * 512, [(1, 512)]), start=(c == 0), stop=(c == 7))
            for g in range(2):
                src = self.bank(b0 + g)
                self.act(sq(0, [(1, 512)]), src(0, [(1, 512)]), AF.Square)
                self.red(st(g * 8, [(1, 8)]), sq(0, [(64, 8), (1, 64)]))
            self.act(st(16, [(1, 16)]), st(0, [(1, 16)]), AF.Ln, scale=1.0 / 64, bias=eps)
            self.act(st(32, [(1, 16)]), st(16, [(1, 16)]), AF.Exp, scale=-0.5)
            for g in range(2):
                src = self.bank(b0 + g)
                xx = tmpr(0, [(64, 8), (1, 64)])
                self.tt(xx, src(0, [(64, 8), (1, 64)]), st(32 + g * 8, [(1, 8), (0, 64)]), ALU.mult)
                self.rope(dasm(g * 512, [(64, 8), (1, 64)]), tmpr, 8, 64, ropeG[g], t * 128, t * 128 + 64, 512, eng=PREP_ENG)
            for i in range(8):
                self.tr(self.bankb(b0 + 3)(i * 128, [(1, 128)]), dasm(i * 128, [(1, 128)]))
            self.cp(dqT(tb, [(8192, 2), (2048, 4), (1, 128)]), self.bankb(b0 + 3)(0, [(512, 2), (128, 4), (1, 128)]), eng="act")
            self.cp(vB(t * 4 * 130, [(130, 4), (1, 128)]), self.bank(b0 + 2)(0, [(128, 4), (1, 128)]), eng="act")


        if ILV_1B:
            L = [self.capture(tile1b, t) for t in range(NT)]
            SPLIT = 30
            self.interleave([L[0][:SPLIT]])
            for t in range(NT):
                nxt = L[t + 1][:SPLIT] if t + 1 < NT else []
                self.interleave([L[t][SPLIT:], nxt])
        else:
            for t in range(NT):
                tile1b(t)

        if "dqT" in DUMPS:
            self.dump("dqT", dqT(0, [(1, 8192)]), [128, 8192])
            self.dump("dkT", dkT(0, [(1, 8192)]), [128, 8192])
            self.dump("vB", vB(0, [(1, 16 * 4 * 130)]), [128, 16 * 4 * 130])
        if STOP == "p1b_prep":
            return

        self.load_w(wout, self.wout_d, 0, 8, 0, 1024, D, 1024)
        self.wb_next = set(range(8))

        for c in range(4):
            for il in range(4):
                t = c * 4 + il
                self.dma(xres(t * 1024, [(1, 1024)]), self.dram(self.x_d, t * 128 * D, [(D, 128), (1, D)]))
            def E1(h):
                ob, stb = obs[h % 2], stbs[h % 2]
                for k in range(4):
                    self.cp(ocp(k * 258, [(1, 258)]), self.bank(4 + k)(0, [(1, 258)]), eng="dve")
                for m in range(2):
                    for bi in range(2):
                        self.recip(rinv(m * 4 + bi * 2, [(1, 2)]), ocp((2 * m + bi) * 258 + 128, [(129, 2)]))
                self.ts(rinv(4, [(1, 4)]), rinv(4, [(1, 4)]), self.neg_lam, ALU.mult)
                for bi in range(2):
                    o1 = ocp(bi * 258, [(129, 2), (1, 128)])
                    o2 = ocp((2 + bi) * 258, [(129, 2), (1, 128)])
                    t_ = tmpo(bi * 256, [(128, 2), (1, 128)])
                    ob_ = ob(bi * 256, [(128, 2), (1, 128)])
                    self.tt(t_, o2, rinv(4 + bi * 2, [(1, 2), (0, 128)]), ALU.mult)
                    self.tt(ob_, o1, rinv(bi * 2, [(1, 2), (0, 128)]), ALU.mult)
                    self.tt(ob_, ob_, t_, ALU.add)
                self.tt(tmpo(0, [(1, 512)]), ob(0, [(1, 512)]), ob(0, [(1, 512)]), ALU.mult)
                self.red(stb(0, [(1, 4)]), tmpo(0, [(128, 4), (1, 128)]))

            def E2(h):
                ob, stb = obs[h % 2], stbs[h % 2]
                self.act(stb(4, [(1, 4)]), stb(0, [(1, 4)]), AF.Ln, scale=1.0 / 128, bias=eps)
                self.act(stb(8, [(1, 4)]), stb(4, [(1, 4)]), AF.Exp, scale=-0.5, bias=self.cst(1, [(1, 1)]))
                self.tt(ob(0, [(128, 4), (1, 128)]), ob(0, [(128, 4), (1, 128)]), stb(8, [(1, 4), (0, 128)]), ALU.mult)
                self.tt(ocat(h * 128, [(512, 4), (1, 128)]), ob(0, [(128, 4), (1, 128)]),
                        bvec(192, [(0, 4), (1, 128)]), ALU.mult)

            for h in range(4):
                self.diff_unit(c, h, dqT, dkT, vB, PT)
                E1(h)
                if h > 0:
                    E2(h - 1)
            E2(3)
            for il in range(4):
                for fc in range(4):
                    self.tr(self.bankb(il % 2)(fc * 128, [(1, 128)]), ocat(il * 512 + fc * 128, [(1, 128)]))
                self.cp(obT(il * 128, [(512, 4), (1, 128)]), self.bankb(il % 2)(0, [(128, 4), (1, 128)]), eng="dve")
            for il in range(4):
                t = c * 4 + il
                for half in range(2):
                    pb = self.bank(2 + half)
                    for fc in range(8):
                        lhsT = oaT(fc * 2048 + t * 128, [(1, 128)]) if fc < 4 else obT((fc - 4) * 512 + il * 128, [(1, 128)])
                        self.mm(pb(0, [(1, 512)]), lhsT, wout(fc * 1024 + half * 512, [(1, 512)]),
                                start=(fc == 0), stop=(fc == 7))
                    xr = xres(t * 1024 + half * 512, [(1, 512)])
                    self.tt(xr, pb(0, [(1, 512)]), xr, ALU.add)
        if "x1" in DUMPS:
            self.dump("x1", xres(0, [(1, 16384)]), [128, 16384])

    def phase2(self, R_X, R_H, R_OA):
        KB = 1024
        xres = self.f32(R_X)
        hT = self.hT
        pvec = self.pvec
        GF = 4
        groups = [(g * GF, min(GF, NF - g * GF)) for g in range((NF + GF - 1) // GF)]
        o = R_OA
        wg = [self.bf(o), self.bf(o + 8 * KB)]; o += 16 * KB
        wu = [self.bf(o), self.bf(o + 8 * KB)]; o += 16 * KB
        wd = [self.bf(o), self.bf(o + 8 * KB)]; o += 16 * KB
        actb = self.bf(o); o += GF * 2048 * 2
        t0 = [self.f32(o), self.f32(o + 2 * KB)]; o += 4 * KB
        sg = [self.f32(o), self.f32(o + 2 * KB)]; o += 4 * KB
        halo = self.f32(o); o += NF * 2 * 4
        xb = [self.bf(o), self.bf(o + 2 * KB)]; o += 4 * KB
        st = self.f32(o); o += 48 * 4
        assert o <= 188 * KB, o
        self.memset(halo(0, [(1, NF * 2)]), 0.0)

        def load_group(gi):
            f0, nf = groups[gi]
            b = gi % 2
            self.load_w(wg[b], self.wg_d, 0, 8, f0 * 128, nf * 128, DFF, 512)
            self.load_w(wu[b], self.wu_d, 0, 8, f0 * 128, nf * 128, DFF, 512)
            self.load_w(wd[b], self.wd_d, f0 * 128, nf, 0, 1024, D, 1024)

        load_group(0)
        self.wb_next = set(range(8))
        for t in range(NT):
            self.norm_tile(t, xres(t * 1024, [(1, 1024)]), hT, 13, xb[t % 2], st, 6 + t % 2)
        for gi, (f0, nf) in enumerate(groups):
            b = gi % 2
            if gi + 1 < len(groups):
                load_group(gi + 1)
            for tc in range(4):
                for j in range(nf):
                    f = f0 + j
                    k = (tc * nf + j) % 2
                    gb = self.bank(k)
                    ub = self.bank(2 + k)
                    for c in range(8):
                        self.mm(gb(0, [(1, 512)]), wg[b](c * 512 + j * 128, [(1, 128)]),
                                hT(c * 2048 + tc * 512, [(1, 512)]), start=(c == 0), stop=(c == 7))
                    for c in range(8):
                        self.mm(ub(0, [(1, 512)]), wu[b](c * 512 + j * 128, [(1, 128)]),
                                hT(c * 2048 + tc * 512, [(1, 512)]), start=(c == 0), stop=(c == 7))
                    a = t0[k]
                    w0 = pvec(21 + f, [(1, 1)])
                    w1 = pvec(43 + f, [(1, 1)])
                    w2 = pvec(65 + f, [(1, 1)])
                    bb = pvec(87 + f, [(1, 1)])
                    self.act(a(0, [(1, 512)]), gb(0, [(1, 512)]), AF.Identity, scale=w2, bias=bb)
                    self.stt(a(1, [(1, 511)]), gb(0, [(1, 511)]), w1, a(1, [(1, 511)]), ALU.mult, ALU.add)
                    self.stt(a(0, [(1, 1)]), halo(f * 2 + 1, [(1, 1)]), w1, a(0, [(1, 1)]), ALU.mult, ALU.add)
                    self.stt(a(2, [(1, 510)]), gb(0, [(1, 510)]), w0, a(2, [(1, 510)]), ALU.mult, ALU.add)
                    self.stt(a(0, [(1, 2)]), halo(f * 2, [(1, 2)]), w0, a(0, [(1, 2)]), ALU.mult, ALU.add)
                    self.cp(halo(f * 2, [(1, 2)]), gb(510, [(1, 2)]), eng="dve")
                    self.act(sg[k](0, [(1, 512)]), a(0, [(1, 512)]), AF.Silu)
                    self.tt(actb(j * 2048 + tc * 512, [(1, 512)]), ub(0, [(1, 512)]), sg[k](0, [(1, 512)]), ALU.mult)
            for t in range(NT):
                for half in range(2):
                    pb = self.bank(4 + (t * 2 + half) % 4)
                    for j in range(nf):
                        self.mm(pb(0, [(1, 512)]), actb(j * 2048 + t * 128, [(1, 128)]),
                                wd[b](j * 1024 + half * 512, [(1, 512)]), start=(j == 0), stop=(j == nf - 1))
                    xr = xres(t * 1024 + half * 512, [(1, 512)])
                    self.tt(xr, pb(0, [(1, 512)]), xr, ALU.add)
                if gi == len(groups) - 1:
                    self.dma(self.dram(self.y_d, t * 128 * D, [(D, 128), (1, D)]), xres(t * 1024, [(1, 1024)]))


def _rope_table(dim):
    inv = (1.0 / (np.float32(10000.0) ** (np.arange(0, dim, 2, dtype=np.float32) / np.float32(dim)))).astype(np.float32)
    ang = (np.arange(S, dtype=np.float32)[:, None] * inv[None, :]).astype(np.float32)
    c, s = np.cos(ang).astype(np.float32), np.sin(ang).astype(np.float32)
    return np.ascontiguousarray(np.concatenate([c, c, -s, s], axis=1))


def _layout_consts(inp):
    f = lambda k: np.asarray(inp[k], dtype=np.float32)[0]
    pvec = np.zeros((128, 112), np.float32)
    pvec[:, 0:8] = f("attn_norm_g").reshape(8, 128).T
    pvec[:, 8:11] = f("q_a_norm_g").reshape(3, 128).T
    pvec[:, 11:13] = f("kv_a_norm_g").reshape(2, 128).T
    pvec[:, 13:21] = f("ffn_norm_g").reshape(8, 128).T
    cw = f("conv_w")
    for j in range(3):
        pvec[:, 21 + 22 * j: 43 + 22 * j] = cw[j].reshape(22, 128).T
    pvec[:, 87:109] = f("conv_b").reshape(22, 128).T
    pvec[:, 109] = 1.0
    pvec[:64, 109] = f("mla_q_norm_g")[:64]
    pvec[:, 110] = 1.0
    pvec[:64, 110] = f("mla_k_norm_g")[:64]
    bv = np.zeros((576,), np.float32)
    bv[0:32] = f("mla_q_norm_g")[64:]
    bv[32:64] = f("mla_k_norm_g")[64:]
    bv[64:128] = f("diff_q_norm_g")
    bv[128:192] = f("diff_k_norm_g")
    bv[192:320] = f("diff_subln_g")
    bv[320:384] = f("lambda_q1")
    bv[384:448] = f("lambda_k1")
    bv[448:512] = f("lambda_q2")
    bv[512:576] = f("lambda_k2")
    bvec = np.ascontiguousarray(np.broadcast_to(bv[None, :], (128, 576)))
    return pvec, bvec


_CACHE = {}


def _get_nc():
    key = (STOP, tuple(DUMPS))
    if key not in _CACHE:
        b = Builder()
        nc = b.build()
        _CACHE[key] = (nc, b)
    return _CACHE[key]


def kernel(**inp):
    nc, b = _get_nc()
    x = np.asarray(inp["x"], dtype=np.float32)
    pvec, bvec = _layout_consts(inp)
    ident = np.eye(128, dtype=np.float32).astype(ml_dtypes.bfloat16)
    kk = np.arange(128)[:, None]
    qq = np.arange(128)[None, :]
    maskT = np.where(qq >= kk, 0.0, NEG).astype(np.float32).astype(ml_dtypes.bfloat16)
    shared = {
        "w_in": np.ascontiguousarray(np.asarray(inp["w_in"], np.float32)[0]),
        "w_q_up": np.ascontiguousarray(np.asarray(inp["w_q_up"], np.float32)[0]),
        "w_kv_up": np.ascontiguousarray(np.asarray(inp["w_kv_up"], np.float32)[0]),
        "w_out": np.ascontiguousarray(np.asarray(inp["w_out"], np.float32)[0]),
        "w_gate": np.ascontiguousarray(np.asarray(inp["w_gate"], np.float32)[0]),
        "w_up": np.ascontiguousarray(np.asarray(inp["w_up"], np.float32)[0]),
        "w_down": np.ascontiguousarray(np.asarray(inp["w_down"], np.float32)[0]),
        "pvec": pvec, "bvec": bvec, "ropeA": _rope_table(32), "ropeB": _rope_table(64),
        "ident": ident, "maskT": maskT,
    }
    in_maps = []
    for i in range(8):
        m = dict(shared)
        m["x"] = np.ascontiguousarray(x[i])
        in_maps.append(m)
    res = run_bass_kernel_spmd(nc, in_maps, core_ids=list(range(8)))
    kernel.last = res
    if STOP is not None:
        return res
    return np.stack([np.asarray(r["y"], dtype=np.float32) for r in res.results], axis=0)
```
